# Optimizing a Trainium2 kernel written in Bass

```python
import math
import jax, jax.numpy as jnp
from jax import lax
import numpy as np

D_MODEL = 1024
BATCH = 2
SEQ = 8192
DEPTH = 2

RMS_EPS = 1e-6
LN_EPS = 1e-5
MIX_WIDTH = D_MODEL
POOL_WIDTH = MIX_WIDTH // 2
POOL_WINDOWS = (2, 4, 8, 16)
POOL_GROUPS = len(POOL_WINDOWS)
POOL_GROUP_DIM = POOL_WIDTH // POOL_GROUPS
SSM_WIDTH = MIX_WIDTH - POOL_WIDTH
SSM_GROUP_DIM = 16
SSM_GROUPS = SSM_WIDTH // SSM_GROUP_DIM
SSM_STATE = 64
DT_MIN = 0.001
DT_MAX = 0.1
CONV_CHANNELS = D_MODEL
CONV_KERNEL = 31
N_EVEN = (DEPTH + 1) // 2
N_ODD = DEPTH // 2

kernel_name = "hybrid_pool_s5_conformer_gated"


def _rmsnorm(x, g):
    x32 = x.astype(jnp.float32)
    y = x32 * lax.rsqrt(jnp.mean(x32 * x32, axis=-1, keepdims=True) + RMS_EPS)
    return (y * g.astype(jnp.float32)).astype(x.dtype)


def _multiscale_pool(u, pool_w, pool_scale):
    b, s, _ = u.shape
    u32 = u.astype(jnp.float32).reshape(b, s, POOL_GROUPS, POOL_GROUP_DIM)
    csum = jnp.cumsum(u32, axis=1)
    pos = jnp.arange(1, s + 1, dtype=jnp.float32)[None, :, None]
    outs = []
    for g, w in enumerate(POOL_WINDOWS):
        c = csum[:, :, g, :]
        lag = jnp.pad(c, ((0, 0), (w, 0), (0, 0)))[:, :s, :]
        outs.append((c - lag) / jnp.minimum(pos, float(w)) - u32[:, :, g, :])
    pooled = jnp.stack(outs, axis=2)
    mixed = jnp.einsum('bsgc,gcd->bsgd', pooled, pool_w.astype(jnp.float32))
    return (mixed.reshape(b, s, POOL_WIDTH) * pool_scale.astype(jnp.float32)).astype(u.dtype)


def _scan_combine(left, right):
    a1r, a1i, b1r, b1i = left
    a2r, a2i, b2r, b2i = right
    ar = a2r * a1r - a2i * a1i
    ai = a2r * a1i + a2i * a1r
    br = a2r * b1r - a2i * b1i + b2r
    bi = a2r * b1i + a2i * b1r + b2i
    return (ar, ai, br, bi)


def _s5(u, log_dt, a_re, a_im, b_re, b_im, c_re, c_im, d_skip, w_glu):
    bsz, s, _ = u.shape
    f32 = jnp.float32
    u32 = u.astype(f32).reshape(bsz, s, SSM_GROUPS, SSM_GROUP_DIM)
    dt = jnp.exp(log_dt.astype(f32))[:, None]
    ar = a_re.astype(f32)
    ai = a_im.astype(f32)
    mag = jnp.exp(ar * dt)
    ang = ai * dt
    abar_re = mag * jnp.cos(ang)
    abar_im = mag * jnp.sin(ang)
    den = ar * ar + ai * ai
    nr = abar_re - 1.0
    ni = abar_im
    k_re = (nr * ar + ni * ai) / den
    k_im = (ni * ar - nr * ai) / den
    br = b_re.astype(f32)
    bi = b_im.astype(f32)
    bb_re = k_re[..., None] * br - k_im[..., None] * bi
    bb_im = k_re[..., None] * bi + k_im[..., None] * br
    bu_re = jnp.einsum('bsgh,gph->bsgp', u32, bb_re)
    bu_im = jnp.einsum('bsgh,gph->bsgp', u32, bb_im)
    shape = bu_re.shape
    _, _, x_re, x_im = lax.associative_scan(
        _scan_combine,
        (jnp.broadcast_to(abar_re, shape), jnp.broadcast_to(abar_im, shape), bu_re, bu_im),
        axis=1)
    y = (jnp.einsum('bsgp,ghp->bsgh', x_re, c_re.astype(f32))
         - jnp.einsum('bsgp,ghp->bsgh', x_im, c_im.astype(f32)))
    y = y.reshape(bsz, s, SSM_WIDTH) + d_skip.astype(f32) * u32.reshape(bsz, s, SSM_WIDTH)
    gv = jnp.einsum('bsc,ce->bse', y, w_glu.astype(f32))
    val, gate = jnp.split(gv, 2, axis=-1)
    return (val * jax.nn.sigmoid(gate)).astype(u.dtype)


def _even_mixer(h, w_in, pool_w, pool_scale, log_dt, a_re, a_im, b_re, b_im,
                c_re, c_im, d_skip, w_glu, w_out):
    proj = jnp.einsum('bsd,de->bse', h, w_in)
    u_pool = proj[..., :POOL_WIDTH]
    u_ssm = proj[..., POOL_WIDTH:MIX_WIDTH]
    z = proj[..., MIX_WIDTH:]
    y_pool = _multiscale_pool(u_pool, pool_w, pool_scale)
    y_ssm = _s5(u_ssm, log_dt, a_re, a_im, b_re, b_im, c_re, c_im, d_skip, w_glu)
    y = jnp.concatenate([y_pool, y_ssm], axis=-1) * jax.nn.silu(z)
    return jnp.einsum('bse,ed->bsd', y, w_out)


def _odd_mixer(h, w_in, conv_w, conv_b, ln_g, ln_b, w_out):
    proj = jnp.einsum('bsd,de->bse', h, w_in)
    val = proj[..., :CONV_CHANNELS]
    gt = proj[..., CONV_CHANNELS:2 * CONV_CHANNELS]
    z = proj[..., 2 * CONV_CHANNELS:]
    g = (val * jax.nn.sigmoid(gt)).astype(jnp.float32)
    kern = conv_w.astype(jnp.float32)[:, None, :]
    c = lax.conv_general_dilated(
        g, kern, window_strides=(1,), padding=[(CONV_KERNEL - 1, 0)],
        dimension_numbers=('NWC', 'WIO', 'NWC'), feature_group_count=CONV_CHANNELS)
    c = c + conv_b.astype(jnp.float32)
    mu = jnp.mean(c, axis=-1, keepdims=True)
    cc = c - mu
    var = jnp.mean(cc * cc, axis=-1, keepdims=True)
    c = cc * lax.rsqrt(var + LN_EPS) * ln_g.astype(jnp.float32) + ln_b.astype(jnp.float32)
    y = (jax.nn.silu(c) * jax.nn.silu(z.astype(jnp.float32))).astype(h.dtype)
    return jnp.einsum('bse,ed->bsd', y, w_out)


def setup_inputs(seed: int = 0) -> dict:
    key = jax.random.key(seed)
    ks = jax.random.split(key, 24)
    f32 = jnp.float32

    def nrm(k, shape, scale):
        return jax.random.normal(k, shape, f32) * scale

    n_idx = jnp.arange(SSM_STATE, dtype=f32)
    a_im0 = jnp.broadcast_to(math.pi * n_idx, (N_EVEN, SSM_GROUPS, SSM_STATE))
    return {
        "x": jax.random.normal(ks[0], (BATCH, SEQ, D_MODEL), f32),
        "even_norm": 1.0 + nrm(ks[1], (N_EVEN, D_MODEL), 0.05),
        "even_w_in": nrm(ks[2], (N_EVEN, D_MODEL, 2 * MIX_WIDTH), D_MODEL ** -0.5),
        "pool_w": nrm(ks[3], (N_EVEN, POOL_GROUPS, POOL_GROUP_DIM, POOL_GROUP_DIM), POOL_GROUP_DIM ** -0.5),
        "pool_scale": 1.0 + nrm(ks[4], (N_EVEN, POOL_WIDTH), 0.05),
        "ssm_log_dt": jax.random.uniform(ks[5], (N_EVEN, SSM_GROUPS), f32,
                                         math.log(DT_MIN), math.log(DT_MAX)),
        "ssm_a_re": -0.5 * jnp.exp(nrm(ks[6], (N_EVEN, SSM_GROUPS, SSM_STATE), 0.05)),
        "ssm_a_im": a_im0 + nrm(ks[7], (N_EVEN, SSM_GROUPS, SSM_STATE), 0.01),
        "ssm_b_re": nrm(ks[8], (N_EVEN, SSM_GROUPS, SSM_STATE, SSM_GROUP_DIM), (2 * SSM_GROUP_DIM) ** -0.5),
        "ssm_b_im": nrm(ks[9], (N_EVEN, SSM_GROUPS, SSM_STATE, SSM_GROUP_DIM), (2 * SSM_GROUP_DIM) ** -0.5),
        "ssm_c_re": nrm(ks[10], (N_EVEN, SSM_GROUPS, SSM_GROUP_DIM, SSM_STATE), SSM_STATE ** -0.5),
        "ssm_c_im": nrm(ks[11], (N_EVEN, SSM_GROUPS, SSM_GROUP_DIM, SSM_STATE), SSM_STATE ** -0.5),
        "ssm_d": nrm(ks[12], (N_EVEN, SSM_WIDTH), 1.0),
        "ssm_w_glu": nrm(ks[13], (N_EVEN, SSM_WIDTH, 2 * SSM_WIDTH), SSM_WIDTH ** -0.5),
        "even_w_out": nrm(ks[14], (N_EVEN, MIX_WIDTH, D_MODEL), MIX_WIDTH ** -0.5),
        "odd_norm": 1.0 + nrm(ks[15], (N_ODD, D_MODEL), 0.05),
        "odd_w_in": nrm(ks[16], (N_ODD, D_MODEL, 3 * CONV_CHANNELS), D_MODEL ** -0.5),
        "conv_w": nrm(ks[17], (N_ODD, CONV_KERNEL, CONV_CHANNELS), CONV_KERNEL ** -0.5),
        "conv_b": nrm(ks[18], (N_ODD, CONV_CHANNELS), 0.02),
        "conv_ln_g": 1.0 + nrm(ks[19], (N_ODD, CONV_CHANNELS), 0.05),
        "conv_ln_b": nrm(ks[20], (N_ODD, CONV_CHANNELS), 0.02),
        "odd_w_out": nrm(ks[21], (N_ODD, CONV_CHANNELS, D_MODEL), CONV_CHANNELS ** -0.5),
        "final_norm": 1.0 + nrm(ks[22], (D_MODEL,), 0.05),
    }


def reference(x, even_norm, even_w_in, pool_w, pool_scale, ssm_log_dt, ssm_a_re, ssm_a_im,
              ssm_b_re, ssm_b_im, ssm_c_re, ssm_c_im, ssm_d, ssm_w_glu, even_w_out,
              odd_norm, odd_w_in, conv_w, conv_b, conv_ln_g, conv_ln_b, odd_w_out, final_norm):
    for i in range(DEPTH):
        j = i // 2
        if i % 2 == 0:
            h = _rmsnorm(x, even_norm[j])
            x = x + _even_mixer(h, even_w_in[j], pool_w[j], pool_scale[j], ssm_log_dt[j],
                                ssm_a_re[j], ssm_a_im[j], ssm_b_re[j], ssm_b_im[j],
                                ssm_c_re[j], ssm_c_im[j], ssm_d[j], ssm_w_glu[j], even_w_out[j])
        else:
            h = _rmsnorm(x, odd_norm[j])
            x = x + _odd_mixer(h, odd_w_in[j], conv_w[j], conv_b[j], conv_ln_g[j],
                               conv_ln_b[j], odd_w_out[j])
    return _rmsnorm(x, final_norm)
```

```python
import math
import numpy as np
import concourse.bass as bass
import concourse.mybir as mybir
from concourse.bass_utils import run_bass_kernel_spmd

F32 = mybir.dt.float32
BF16 = mybir.dt.bfloat16
AF = mybir.ActivationFunctionType
ALU = mybir.AluOpType

NT = 17
T = NT * 128
PAD = 2048
NLEV = 12
TWO_PI = 2.0 * math.pi
DEBUG = False
_MARKS = []
USE_CC = False


class Sched:
    ENG = ("pe", "act", "dve", "pool", "sp")
    NDMA = 12

    def __init__(self, sems, dma_sems):
        self.sem = dict(zip(self.ENG, sems))
        self.dma_sems = dma_sems
        self.lists = {e: [] for e in self.ENG}
        self.count = {e: 0 for e in self.ENG}
        self.ndma = 0
        self.last_w = {}
        self.readers = {}
        self.seen = {e: {} for e in self.ENG}
        self.last_ev = {}
        self.dma_last = {}

    def _need(self, eng, ev, waits):
        if ev is None:
            return
        key, sem, val, peng = ev
        if peng == eng and eng == "pe":
            return
        if self.seen[eng].get(key, 0) >= val:
            return
        cur = waits.get(key)
        if cur is None or cur[1] < val:
            waits[key] = (sem, val)

    def op(self, eng, fn, reads=(), writes=(), dma=False):
        waits = {}
        for k in reads:
            self._need(eng, self.last_w.get(k), waits)
        for k in writes:
            self._need(eng, self.last_w.get(k), waits)
            for ev in self.readers.get(k, ()):
                self._need(eng, ev, waits)
        if dma:
            j = self.ndma
            self.ndma += 1
            slot = j % self.NDMA
            s = self.dma_sems[slot]
            val = 16 * (j // self.NDMA + 1)
            key = ("dma", slot)
            if val > 16:
                self._need(eng, (key, s, val - 16, "dmaq"), waits)
            ev = (key, s, val, "dmaq")
            inc = (s, 16)
            self.dma_last[slot] = ev
        else:
            self.count[eng] += 1
            ev = (("eng", eng), self.sem[eng], self.count[eng], eng)
            inc = (self.sem[eng], 1)
            self.last_ev[eng] = ev
        for key, (sem, val) in waits.items():
            self.lists[eng].append(("wait", sem, val))
            self.seen[eng][key] = val
        self.lists[eng].append(("op", fn, inc))
        for k in reads:
            self.readers.setdefault(k, []).append(ev)
        for k in writes:
            self.last_w[k] = ev
            self.readers[k] = []
        return ev

    def barrier(self):
        evs = list(self.last_ev.values()) + list(self.dma_last.values())
        for eng in self.ENG:
            waits = {}
            for ev in evs:
                self._need(eng, ev, waits)
            for key, (sem, val) in waits.items():
                self.lists[eng].append(("wait", sem, val))
                self.seen[eng][key] = val

    def replay(self, eng, e):
        for item in self.lists[eng]:
            if item[0] == "wait":
                e.wait_ge(item[1], item[2])
            else:
                inst = item[1](e)
                inst.then_inc(item[2][0], item[2][1])


def token_blocks(bs):
    out = []
    t = 0
    while t < T:
        n = min(bs, T - t)
        out.append((t, n))
        t += n
    return out


V_PSC, V_SSD, V_CB, V_LG, V_LB, V_CW, V_INV, V_RM, V_SG, V_SEL = 0, 4, 8, 16, 24, 32, 280, 344, 352, 353
V_G0, V_G1 = 377, 385
NV = 393


def build_nc():
    from contextlib import ExitStack
    nc = bass.Bass("TRN2", target_bir_lowering=False)
    st = ExitStack()
    dram = lambda n, s, k="ExternalInput", d=F32: nc.dram_tensor(n, s, d, kind=k).ap()
    xin = dram("xin", [T, 1024])
    xpre = dram("xpre", [3, 2048, 1024])
    w_in0 = dram("w_in0", [1024, 2048])
    w_glu = dram("w_glu", [512, 1024])
    w_out0 = dram("w_out0", [1024, 1024])
    pool_w = dram("pool_w", [4, 128, 128])
    w_in1 = dram("w_in1", [1024, 3072])
    w_out1 = dram("w_out1", [1024, 1024])
    gains = dram("gains", [3, 128, 1024])
    vecs_d = dram("vecs", [128, NV])
    ssmA_d = dram("ssmA", [128, 3, 32])
    bp1_d = dram("bp1", [128, 32, 128])
    bp2_d = dram("bp2", [128, 32, 128])
    cp_d = dram("cp", [128, 32, 128])
    bp1c_d = dram("bp1c", [128, 32, 16])
    bp2c_d = dram("bp2c", [128, 32, 16])
    cpc_d = dram("cpc", [128, 32, 16])
    cp2c_d = dram("cp2c", [128, 32, 16])
    ident_d = dram("ident", [128, 128])
    jswap_d = dram("jswap", [128, 128])
    out_d = dram("out", [2048, 1024], "ExternalOutput")
    ebuf = nc.dram_tensor("ebuf", [128, 32], F32).ap()
    egat = nc.dram_tensor("egat", [8 * 128, 32], F32).ap()
    taps = {}

    def sb(name, shape, dt, off):
        return nc.alloc_sbuf_tensor_at(name, shape, dt, offset=16576 + off)

    RA, RB, RC, RD, RE, RF, RG, RH = 0, 69632, 104448, 139264, 156672, 174080, 190464, 196608
    s_t = sb("s_t", [128, 8, T], BF16, RB)
    yg = sb("yg", [128, 8, T], BF16, RC)
    U = sb("U", [128, 4, T], BF16, RD)
    ysm = sb("ysm", [128, 4, T], BF16, RE)
    stg = [sb("stg0", [128, 2048], F32, RF), sb("stg1", [128, 2048], F32, RF + 8192)]
    ident_bf = sb("ident_bf", [128, 128], BF16, RG)
    ident_f = sb("ident_f", [128, 128], F32, RG + 256)
    jswap_f = sb("jswap_f", [128, 128], F32, RG + 768)
    ones_bf = sb("ones_bf", [128, 128], BF16, RG + 1280)
    vecs = sb("vecs_t", [128, NV], F32, RG + 1536)
    ss = sb("ss", [128, 8], F32, RG + 3136)
    rs = sb("rs", [128, 8], F32, RG + 3168)
    rt = sb("rt", [128, 8], F32, RG + 3328)
    w_in0_t = sb("w_in0_t", [128, 8, 2048], BF16, RA)
    upool = sb("upool", [128, 4, 16 + T], BF16, RA + 32768)
    hT = sb("hT", [128, 8, 512], BF16, RA + 50304)
    xt = [sb("xt0", [128, 1024], F32, RA + 58496), sb("xt1", [128, 1024], F32, RA + 62592)]
    h_t = sb("h_t", [128, 1024], BF16, RA + 66688)
    h_t2 = sb("h_t2", [128, 1024], BF16, RE + 12416)
    pool_w_t = sb("pool_w_t", [128, 4, 128], BF16, RE)
    gain_t = sb("gain_t", [128, 1024], F32, RE + 1024)
    sq = sb("sq", [128, 1024], F32, RE + 5120)
    wa = sb("wa", [128, 528], BF16, RE + 9216)
    wb = sb("wb", [128, 528], BF16, RE + 10272)
    pl = sb("pl", [128, 512], BF16, RE + 11328)
    ptmp = sb("ptmp", [128, 16], F32, RE + 12352)
    X = [sb("X0", [128, PAD + T], BF16, RA), sb("X1", [128, PAD + T], BF16, RA + 8448)]
    Z = sb("Z", [128, 2304], BF16, RA + 16896)
    bopp = [sb("bopp0", [128, 128], BF16, RA + 21504), sb("bopp1", [128, 128], BF16, RA + 21760)]
    btmp = sb("btmp", [128, 128], F32, RA + 22016)
    bl = sb("bl", [128, 128], BF16, RA + 22528)
    mtmp = sb("mtmp", [128, 128], F32, RA + 22784)
    mtmp2 = sb("mtmp2", [128, 128], F32, RH + 11264)
    mt4 = sb("mt4", [128, NLEV, 128], BF16, RH + 12288)
    PRM = RA + 32768
    prm = lambda name, i, n=32: sb(name, [128, n], F32, PRM + 128 * i)
    a_re, a_im, ldt = prm("a_re", 0), prm("a_im", 1), prm("ldt", 2)
    dt_t, t0_, t1_, t2_, t3_ = prm("dt_t", 3), prm("t0_", 4), prm("t1_", 5), prm("t2_", 6), prm("t3_", 7)
    k_re, nkims = prm("k_re", 8), prm("nkims", 9)
    Eloc = prm("Eloc", 10)
    Xin = sb("Xin", [128, 32], F32, RG + 3200)
    Vh = sb("Vh", [128, 32, 10, 2], F32, RG + 3392)
    Eb = sb("Eb", [128, 64], BF16, RG + 5952)
    qi = sb("qi", [128, 32], mybir.dt.int32, PRM + 6784)
    qf = sb("qf", [128, 32], F32, PRM + 6912)
    c2ims, c1ims = prm("c2ims", 12), prm("c1ims", 13)
    Pre = sb("Pre", [128, 13, 32], F32, PRM + 128 * 14)
    Pim = sb("Pim", [128, 13, 32], F32, PRM + 128 * 14 + 1664)
    v2 = sb("v2", [128, 13, 32], F32, PRM + 128 * 14 + 3328)
    EG = sb("EG", [128, 8, 32], F32, PRM + 6784)
    EGs = sb("EGs", [128, 8, 32], F32, PRM + 7808)
    Yacc = sb("Yacc", [128, 4, T], F32, RA + 32768)
    COP = sb("COP", [128, 32, 128], BF16, RH)
    MOP = sb("MOP", [128, NLEV, 128], BF16, RH + 8192)
    w_glu_t = sb("w_glu_t", [128, 4, 1024], BF16, RA)
    sgt = [sb("sgt0", [128, 512], F32, RA + 8192), sb("sgt1", [128, 512], F32, RA + 10240)]
    w_out0_t = sb("w_out0_t", [128, 8, 1024], BF16, RD)
    xt4 = [sb("xt40", [128, 1024], F32, RE), sb("xt41", [128, 1024], F32, RE + 4096)]
    x1 = sb("x1", [128, NT, 1024], F32, RA)
    w_in1_t = sb("w_in1_t", [128, 8, 3072], BF16, RB)
    w_out1_t = sb("w_out1_t", [128, 8, 1024], BF16, 118784)
    TD = 7
    diag2 = [sb("diag0", [128, 31 - TD, 128], BF16, 135168), sb("diag1", [128, 31 - TD, 128], BF16, RH + 8192)]
    cacc = [sb("cacc0", [128, 256], F32, 141824), sb("cacc1", [128, 256], F32, RH + 14848)]
    hT1 = sb("hT1", [128, 8, 256], BF16, 143104)
    g_t = sb("g_t", [128, 8, 288], BF16, 147200)
    s1 = sb("s1", [128, 8, 256], BF16, 151808)
    gain1 = sb("gain1", [128, 1024], F32, 155904)
    gainf = sb("gainf", [128, 1024], F32, 160000)
    h1 = sb("h1", [128, 1024], BF16, 164096)
    mean = sb("mean", [128, 256], F32, 166144)
    rstd = sb("rstd", [128, 256], F32, 167168)
    sg1 = [sb("sg10", [128, 256], F32, 168192), sb("sg11", [128, 256], F32, 169216)]
    cbf = [sb("cbf0", [128, 256], BF16, 170240), sb("cbf1", [128, 256], BF16, 170752)]
    csq = [sb("csq0", [128, 256], BF16, 171264), sb("csq1", [128, 256], BF16, 171776)]
    c_t = sb("c_t", [128, 8, 256], F32, RF)
    y1 = sb("y1", [128, 8, 256], BF16, RF + 8192)
    ot = sb("ot", [128, 1024], F32, RF + 12288)
    sq1 = sb("sq1", [128, 1024], BF16, RH)
    h1b = sb("h1b", [128, 1024], BF16, RH + 2048)
    ct = sb("ct", [128, 256], F32, RH + 6144)
    ct2 = sb("ct2", [128, 256], F32, RH + 7168)

    CW, PADC, NLC = 272, 256, 10
    U_d = sb("U_d", [128, 4, 8, CW], BF16, RD)
    U_pd = sb("U_pd", [128, 4, 8, 768], BF16, RB)
    Xs_all = sb("Xs_all", [128, 8, 288], BF16, RA)
    BLx = [sb("BLx0", [128, 8, 128], BF16, RA + 4608), sb("BLx1", [128, 8, 128], BF16, RA + 6656),
           sb("BLx2", [128, 8, 128], BF16, RA + 66560), sb("BLx3", [128, 8, 128], BF16, RA + 28288)]
    mt4c = sb("mt4c", [128, NLC, 128], BF16, RA + 8704)
    t_a = sb("t_a", [128, 9, 16], F32, RA + 8704)
    t_b = sb("t_b", [128, 9, 16], F32, RA + 9280)
    MSL = []
    for i_ in range(3):
        o_ = RA + 9856 + i_ * 2176
        MSL.append(dict(Xa=sb("mXa%d" % i_, [128, 544], BF16, o_), Xb=sb("mXb%d" % i_, [128, 544], BF16, o_ + 1088)))
    MOPS = []
    for i_ in range(6):
        o_ = RA + 16384 + i_ * 4608
        MOPS.append(dict(SOPe=sb("mSOPe%d" % i_, [128, 8, 128], BF16, o_), MOPc=sb("mMOPc%d" % i_, [128, NLC, 128], BF16, o_ + 2048)))
    PSL = []
    POPS = []
    for i_ in range(4):
        o_ = RD + i_ * 8704
        POPS.append(dict(SOPe=sb("pSOPe%d" % i_, [128, 8, 128], BF16, o_), MOPc=sb("pMOPc%d" % i_, [128, NLC, 128], BF16, o_ + 2048)))
        PSL.append(dict(T0=sb("pT0%d" % i_, [128, 1024], BF16, o_ + 4608), Tb=sb("pTb%d" % i_, [128, 512], BF16, o_ + 6656),
                        Tc=sb("pTc%d" % i_, [128, 512], BF16, o_ + 7680)))
    for i_ in range(4):
        o_ = RA + 9856 + i_ * 4608
        POPS.append(dict(SOPe=sb("pSOPe%d" % (4 + i_), [128, 8, 128], BF16, o_), MOPc=sb("pMOPc%d" % (4 + i_), [128, NLC, 128], BF16, o_ + 2048)))
    COPs = sb("COPs", [128, 8, 9, 128], BF16, RA + 44032)
    B0pad = sb("B0pad", [128, 8, 128], BF16, RA + 62464)
    KOP = sb("KOP", [128, 8, 128], BF16, RA + 64512)
    bp1c = sb("bp1c", [128, 32, 16], F32, RH)
    bp2c = sb("bp2c", [128, 32, 16], F32, RH + 2048)
    cpc = sb("cpc", [128, 32, 16], F32, RH + 4096)
    cp2c = sb("cp2c", [128, 32, 16], F32, RH + 6144)
    Are = sb("Are", [128, 9, 32], F32, RH + 8192)
    Aim = sb("Aim", [128, 9, 32], F32, RH + 9344)
    sAre = sb("sAre", [128, 9, 32], F32, RH + 10496)
    nAim = sb("nAim", [128, 9, 32], F32, RH + 11648)
    kpre = sb("kpre", [128, 8, 32], F32, RH + 12800)
    nkpims = sb("nkpims", [128, 8, 32], F32, RH + 13824)
    k_im = sb("k_im", [128, 32], F32, RH + 14848)
    tmp8 = sb("tmp8", [128, 8, 32], F32, RH + 14976)
    psum = [st.enter_context(nc.psum_tensor("ps%d" % i, [128, 512], F32)) for i in range(8)]
    sems = [st.enter_context(nc.semaphore("s_" + e)) for e in Sched.ENG]
    dsems = [st.enter_context(nc.semaphore("dq%d" % i)) for i in range(Sched.NDMA)]
    S = Sched(sems, dsems)
    psrot = {"i": 0, "set": list(range(8))}
    del _MARKS[:]
    mark = lambda name: _MARKS.append((name, dict(S.count)))

    def PS():
        lst = psrot["set"]
        i = lst[psrot["i"] % len(lst)]
        psrot["i"] += 1
        return i

    rr = {"i": 0, "stg": 0, "rn": 0}

    def evac_eng():
        rr["i"] += 1
        return "act" if rr["i"] % 2 else "dve"

    def copy_op(eng, out, in_, reads, writes):
        if eng == "act":
            S.op("act", lambda e: e.activation(out=out, in_=in_, func=AF.Copy), reads, writes)
        else:
            S.op(eng, lambda e: e.tensor_copy(out=out, in_=in_), reads, writes)

    def tt(eng, out, in0, in1, op, reads, writes):
        S.op(eng, lambda e: e.tensor_tensor(out=out, in0=in0, in1=in1, op=op), reads, writes)

    def ts(eng, out, in0, s1, s2, op0, op1, reads, writes):
        if op1 is None:
            S.op(eng, lambda e: e.tensor_scalar(out=out, in0=in0, scalar1=s1, scalar2=None, op0=op0), reads, writes)
        else:
            S.op(eng, lambda e: e.tensor_scalar(out=out, in0=in0, scalar1=s1, scalar2=s2, op0=op0, op1=op1), reads, writes)

    def stt(eng, out, in0, sc, in1, op0, op1, reads, writes):
        S.op(eng, lambda e: e.scalar_tensor_tensor(out=out, in0=in0, scalar=sc, in1=in1, op0=op0, op1=op1), reads, writes)

    def act(out, in_, func, reads, writes, bias=None, scale=None, accum_out=None):
        kw = {}
        if bias is not None:
            kw["bias"] = bias
        if scale is not None:
            kw["scale"] = scale
        if accum_out is not None:
            kw["accum_out"] = accum_out
        S.op("act", lambda e: e.activation(out=out, in_=in_, func=func, **kw), reads, writes)

    def dma(out, in_, reads, writes, eng="sp"):
        return S.op(eng, lambda e: e.dma_start(out=out, in_=in_), reads, writes, dma=True)

    def load_w(dst, src2d, K, C, key, gcol=None, c_lo=0, k_lo=0, half=0):
        for k in range(k_lo, K):
            for c0 in range(0, C, 2048):
                cw = min(2048, C - c0)
                j = rr["stg"] % 2
                rr["stg"] += 1
                dma(stg[j][:, 0:cw], src2d[k * 128:(k + 1) * 128, c_lo + c0:c_lo + c0 + cw], [], ["stg%d" % j])
                hw = max(0, min(half - c0, cw))
                if hw > 0:
                    g1 = vecs[:, gcol + k:gcol + k + 1] if gcol is not None else 1.0
                    ts("dve", dst[:, k, c0:c0 + hw], stg[j][:, 0:hw], g1, 0.5, ALU.mult, ALU.mult, ["stg%d" % j, "vecs"], [key])
                if hw < cw:
                    o_, i_ = dst[:, k, c0 + hw:c0 + cw], stg[j][:, hw:cw]
                    if gcol is None:
                        copy_op(evac_eng(), o_, i_, ["stg%d" % j], [key])
                    elif evac_eng() == "act":
                        act(o_, i_, AF.Copy, ["stg%d" % j, "vecs"], [key], scale=vecs[:, gcol + k:gcol + k + 1])
                    else:
                        ts("dve", o_, i_, vecs[:, gcol + k:gcol + k + 1], None, ALU.mult, None, ["stg%d" % j, "vecs"], [key])

    def mm_group(bank, n, pairs, reads, col0=0):
        def fn(e):
            last = None
            for i, (l, r) in enumerate(pairs):
                last = e.matmul(psum[bank][:, col0:col0 + n], lhsT=l, rhs=r, start=(i == 0), stop=(i == len(pairs) - 1))
            return last
        S.op("pe", fn, reads, ["ps%d" % bank])

    def tap(name, ap, shape, dt, reads):
        if not DEBUG:
            return
        d = nc.dram_tensor("tap_" + name, shape, dt, kind="ExternalOutput").ap()
        taps[name] = dma(d, ap, reads, [])

    I32 = mybir.dt.int32
    MAGIC = 0x5f3759df

    def rsqrt_dve(y, x, t, yi, xi, keys, iters):
        ts("dve", yi, xi, 1, None, ALU.arith_shift_right, None, keys, keys)
        ts("dve", yi, yi, -1, MAGIC, ALU.mult, ALU.add, keys, keys)
        for _ in range(iters):
            tt("dve", t, y, y, ALU.mult, keys, keys)
            tt("dve", t, t, x, ALU.mult, keys, keys)
            ts("dve", t, t, -0.5, 1.5, ALU.mult, ALU.add, keys, keys)
            tt("dve", y, y, t, ALU.mult, keys, keys)

    ssi, rsi = ss.bitcast(I32), rs.bitcast(I32)

    def rstd_chain(src, srckeys, junk, junkkeys, iters=2, use_act=False):
        c = 4 + rr["rn"] % 4
        rr["rn"] += 1
        sk = "ss%d" % c
        xs, ys, tc = ss[:, c:c + 1], rs[:, c:c + 1], rt[:, c:c + 1]
        S.op("dve", lambda e: e.memset(ss[:, c:c + 1], 0.0), [], [sk])
        act(junk, src, AF.Square, srckeys, [sk] + junkkeys, accum_out=xs)
        ts("dve", xs, xs, 1.0 / 1024, 1e-6, ALU.mult, ALU.add, [sk], [sk])
        if use_act:
            act(ys, xs, AF.Sqrt, [sk], [sk])
            S.op("dve", lambda e: e.reciprocal(out=rs[:, c:c + 1], in_=rs[:, c:c + 1]), [sk], [sk])
            return ys, sk
        ts("dve", rsi[:, c:c + 1], ssi[:, c:c + 1], 1, None, ALU.arith_shift_right, None, [sk], [sk])
        ts("dve", rsi[:, c:c + 1], rsi[:, c:c + 1], -1, MAGIC, ALU.mult, ALU.add, [sk], [sk])
        for _ in range(iters):
            stt("dve", tc, ys, xs, ys, ALU.mult, ALU.mult, [sk], [sk])
            ts("dve", tc, tc, -0.5, 1.5, ALU.mult, ALU.add, [sk], [sk])
            tt("dve", ys, ys, tc, ALU.mult, [sk], [sk])
        return ys, sk

    def rmsnorm(src, srckeys, hout, hkey, use_act=False, scale_eng="act"):
        rcol, rk = rstd_chain(src, srckeys, hout, [hkey], use_act=use_act)
        if scale_eng == "act":
            act(hout, src, AF.Copy, srckeys + [rk], [hkey], scale=rcol)
        else:
            ts("dve", hout, src, rcol, None, ALU.mult, None, srckeys + [rk], [hkey])

    def make_rms_pipe(srcfn, scale_eng, junkbuf):
        st = {}

        def S0(i):
            pre = srcfn(i)[0]
            if pre is not None:
                pre()

        def S1(i):
            pre, src, srckeys, hout, hkey = srcfn(i)
            c = i % 4
            sk = "ss%d" % c
            S.op("dve", lambda e: e.memset(ss[:, c:c + 1], 0.0), [], [sk])
            act(junkbuf[:], src, AF.Square, srckeys, [sk, "junk"], accum_out=ss[:, c:c + 1])
            st[i] = (src, srckeys, hout, hkey, c, sk)

        def S2(i):
            src, srckeys, hout, hkey, c, sk = st[i]
            ts("dve", ss[:, c:c + 1], ss[:, c:c + 1], 1.0 / 1024, 1e-6, ALU.mult, ALU.add, [sk], [sk])
            act(rs[:, c:c + 1], ss[:, c:c + 1], AF.Sqrt, [sk], [sk])

        def S3(i):
            src, srckeys, hout, hkey, c, sk = st.pop(i)
            S.op("dve", lambda e: e.reciprocal(out=rs[:, c:c + 1], in_=rs[:, c:c + 1]), [sk], [sk])
            if scale_eng == "act":
                act(hout, src, AF.Copy, srckeys + [sk], [hkey], scale=rs[:, c:c + 1])
            else:
                ts("dve", hout, src, rs[:, c:c + 1], None, ALU.mult, None, srckeys + [sk], [hkey])

        def prologue(N):
            for i in range(-5, 0):
                step(i, N)

        def step(i, N):
            if 0 <= i + 5 < N:
                S0(i + 5)
            if 0 <= i + 4 < N:
                S1(i + 4)
            if 0 <= i + 3 < N:
                S2(i + 3)
            if 0 <= i + 2 < N:
                S3(i + 2)
        return prologue, step

    def final_norm(src, srckeys, gain_tile, gkey, sqbuf):
        rcol, rk = rstd_chain(src, srckeys, sqbuf[:], ["junk"], use_act=True)
        stt("dve", src, src, rcol, gain_tile[:], ALU.mult, ALU.mult, srckeys + [rk, gkey], srckeys)

    def transpose8(hsrc, hkey, hTdst, off, hTkey):
        for half in range(2):
            b = PS()
            def fn(e, half=half, b=b):
                last = None
                for k in range(4):
                    kk = half * 4 + k
                    last = e.matmul(psum[b][:, k * 128:(k + 1) * 128], lhsT=hsrc[:, kk * 128:(kk + 1) * 128], rhs=ident_bf[:], start=True, stop=True)
                return last
            S.op("pe", fn, [hkey, "ident_bf"], ["ps%d" % b])
            copy_op(evac_eng(), hTdst[:, half * 4:half * 4 + 4, off:off + 128],
                    psum[b][:, :].rearrange("p (a b) -> p a b", a=4), ["ps%d" % b], [hTkey])

    dma(ident_f[:], ident_d, [], ["ident_f"])
    dma(jswap_f[:], jswap_d, [], ["jswap_f"])
    dma(vecs[:], vecs_d, [], ["vecs"])
    copy_op("pool", ident_bf[:], ident_f[:], ["ident_f"], ["ident_bf"])
    S.op("pool", lambda e: e.memset(ones_bf[:], 1.0), [], ["ones_bf"])

    dgs = nc.dram_tensor("dgs", [8, 128, 31 * 128], BF16).ap()
    dstage = [sb("dstage0", [128, 31, 128], BF16, RD), sb("dstage1", [128, 31, 128], BF16, RD + 8192)]

    def diag_gen_fn():
        for m in range(8):
            dsb, dsk = dstage[m % 2], "dstage%d" % (m % 2)
            tt("dve", dsb[:, :, :], bass.AP(ident_f, 0, [[128, 128], [0, 31], [1, 128]]),
               bass.AP(vecs, V_CW + m * 31, [[NV, 128], [1, 31], [0, 128]]), ALU.mult, ["ident_f", "vecs"], [dsk])
            dma(dgs[m], dsb[:, :, :].rearrange("p a b -> p (a b)"), [dsk], ["dgs"])
            yield

    TB = token_blocks(512)
    P = ["prm"]
    sg1c = vecs[:, V_SG:V_SG + 1]
    def build_mop(g):
        idb = bass.AP(ident_f, 0, [[128, 128], [0, NLEV], [1, 128]])
        jsb = bass.AP(jswap_f, 0, [[128, 128], [0, NLEV], [1, 128]])
        v1b = bass.AP(Pre, g, [[13 * 32, 128], [32, NLEV], [0, 128]])
        v2b = bass.AP(v2, g, [[13 * 32, 128], [32, NLEV], [0, 128]])
        tt("dve", mt4[:, :, :], jsb, v2b, ALU.mult, ["jswap_f"] + P, ["mt4"])
        tt("dve", MOP[:, :, :], idb, v1b, ALU.mult, ["ident_f"] + P, ["MOP"])
        tt("dve", MOP[:, :, :], MOP[:, :, :], mt4[:, :, :], ALU.add, ["MOP", "mt4"], ["MOP"])

    def ssm_setup():
        dma(a_re[:], ssmA_d[:, 0, :], [], ["prm"])
        yield
        dma(a_im[:], ssmA_d[:, 1, :], [], ["prm"])
        yield
        dma(ldt[:], ssmA_d[:, 2, :], [], ["prm"])
        yield
        P = ["prm"]
        sg1c = vecs[:, V_SG:V_SG + 1]
        act(dt_t[:], ldt[:], AF.Exp, P, P)
        yield
        tt("dve", t0_[:], a_re[:], dt_t[:], ALU.mult, P, P)
        yield
        act(t0_[:], t0_[:], AF.Exp, P, P)
        yield
        tt("dve", t1_[:], a_im[:], dt_t[:], ALU.mult, P, P)
        yield
        def sin_of(dst, src, shift):
            ts("dve", dst, src, shift, 1.0 / TWO_PI, ALU.add, ALU.mult, P, P)
            copy_op("dve", qi[:], dst, P, P)
            copy_op("dve", qf[:], qi[:], P, P)
            ts("dve", dst, src, shift, None, ALU.add, None, P, P)
            stt("dve", dst, qf[:], -TWO_PI, dst, ALU.mult, ALU.add, P, P)
            ts("dve", qf[:], dst, math.pi, -TWO_PI, ALU.is_gt, ALU.mult, P, P)
            tt("dve", dst, dst, qf[:], ALU.add, P, P)
            ts("dve", qf[:], dst, -math.pi, TWO_PI, ALU.is_lt, ALU.mult, P, P)
            tt("dve", dst, dst, qf[:], ALU.add, P, P)
            act(dst, dst, AF.Sin, P, P)
        sin_of(t2_[:], t1_[:], 0.0)
        yield
        sin_of(t3_[:], t1_[:], 0.5 * math.pi)
        yield
        tt("dve", Pre[:, 0, :], t0_[:], t3_[:], ALU.mult, P, P)
        yield
        tt("dve", Pim[:, 0, :], t0_[:], t2_[:], ALU.mult, P, P)
        yield
        ts("dve", t0_[:], Pre[:, 0, :], -1.0, None, ALU.add, None, P, P)
        yield
        tt("dve", t1_[:], a_re[:], a_re[:], ALU.mult, P, P)
        yield
        tt("dve", t2_[:], a_im[:], a_im[:], ALU.mult, P, P)
        yield
        tt("dve", t1_[:], t1_[:], t2_[:], ALU.add, P, P)
        yield
        S.op("dve", lambda e: e.reciprocal(out=t1_[:], in_=t1_[:]), P, P)
        yield
        tt("dve", t2_[:], t0_[:], a_re[:], ALU.mult, P, P)
        yield
        tt("dve", t3_[:], Pim[:, 0, :], a_im[:], ALU.mult, P, P)
        yield
        tt("dve", t2_[:], t2_[:], t3_[:], ALU.add, P, P)
        yield
        tt("dve", k_re[:], t2_[:], t1_[:], ALU.mult, P, P)
        yield
        tt("dve", t2_[:], Pim[:, 0, :], a_re[:], ALU.mult, P, P)
        yield
        tt("dve", t3_[:], t0_[:], a_im[:], ALU.mult, P, P)
        yield
        tt("dve", t2_[:], t2_[:], t3_[:], ALU.subtract, P, P)
        yield
        tt("dve", t2_[:], t2_[:], t1_[:], ALU.mult, P, P)
        yield
        copy_op("dve", k_im[:], t2_[:], P, P)
        yield
        ts("dve", nkims[:], t2_[:], sg1c, -1.0, ALU.mult, ALU.mult, P + ["vecs"], P)
        yield
        for k in range(12):
            tt("dve", t0_[:], Pre[:, k, :], Pre[:, k, :], ALU.mult, P, P)
            yield
            tt("dve", t1_[:], Pim[:, k, :], Pim[:, k, :], ALU.mult, P, P)
            yield
            tt("dve", Pre[:, k + 1, :], t0_[:], t1_[:], ALU.subtract, P, P)
            yield
            tt("dve", t0_[:], Pre[:, k, :], Pim[:, k, :], ALU.mult, P, P)
            yield
            ts("dve", Pim[:, k + 1, :], t0_[:], 2.0, None, ALU.mult, None, P, P)
            yield
        ts("dve", v2[:, :, :], Pim[:, :, :], sg1c, None, ALU.mult, None, P + ["vecs"], P)
        yield
        ts("dve", c1ims[:], v2[:, 11, :], -1.0, None, ALU.mult, None, P, P)
        yield
        ts("dve", c2ims[:], v2[:, 12, :], -1.0, None, ALU.mult, None, P, P)
        yield
        S.op("dve", lambda e: e.memset(Are[:, 0, :], 1.0), [], P)
        yield
        S.op("dve", lambda e: e.memset(Aim[:, 0, :], 0.0), [], P)
        yield
        copy_op("dve", Are[:, 1, :], Pre[:, 0, :], P, P)
        yield
        copy_op("dve", Aim[:, 1, :], Pim[:, 0, :], P, P)
        yield
        for e_ in range(1, 8):
            tt("dve", t0_[:], Are[:, e_, :], Pre[:, 0, :], ALU.mult, P, P)
            yield
            tt("dve", t1_[:], Aim[:, e_, :], Pim[:, 0, :], ALU.mult, P, P)
            yield
            tt("dve", Are[:, e_ + 1, :], t0_[:], t1_[:], ALU.subtract, P, P)
            yield
            tt("dve", t0_[:], Are[:, e_, :], Pim[:, 0, :], ALU.mult, P, P)
            yield
            tt("dve", t1_[:], Aim[:, e_, :], Pre[:, 0, :], ALU.mult, P, P)
            yield
            tt("dve", Aim[:, e_ + 1, :], t0_[:], t1_[:], ALU.add, P, P)
            yield
        kre_b = bass.AP(k_re, 0, [[32, 128], [0, 8], [1, 32]])
        kim_b = bass.AP(k_im, 0, [[32, 128], [0, 8], [1, 32]])
        tt("dve", kpre[:, :, :], Are[:, 0:8, :], kre_b, ALU.mult, P, P)
        yield
        tt("dve", tmp8[:, :, :], Aim[:, 0:8, :], kim_b, ALU.mult, P, P)
        yield
        tt("dve", kpre[:, :, :], kpre[:, :, :], tmp8[:, :, :], ALU.subtract, P, P)
        yield
        tt("dve", nkpims[:, :, :], Are[:, 0:8, :], kim_b, ALU.mult, P, P)
        yield
        tt("dve", tmp8[:, :, :], Aim[:, 0:8, :], kre_b, ALU.mult, P, P)
        yield
        tt("dve", nkpims[:, :, :], nkpims[:, :, :], tmp8[:, :, :], ALU.add, P, P)
        yield
        ts("dve", nkpims[:, :, :], nkpims[:, :, :], sg1c, -1.0, ALU.mult, ALU.mult, P + ["vecs"], P)
        yield
        ts("dve", sAre[:, :, :], Are[:, :, :], sg1c, None, ALU.mult, None, P + ["vecs"], P)
        yield
        ts("dve", nAim[:, :, :], Aim[:, :, :], -1.0, None, ALU.mult, None, P, P)
        yield
        dma(bp1c[:], bp1c_d, [], P)
        yield
        dma(bp2c[:], bp2c_d, [], P)
        yield
        dma(cpc[:], cpc_d, [], P)
        yield
        dma(cp2c[:], cp2c_d, [], P)
        yield
        tt("dve", Eb[:, :], ident_f[:, 0:64], ident_f[:, 64:128], ALU.add, ["ident_f"], ["Eb"])
        yield
        for (lo_, hi_, s0, s1_) in ((0, 64, Pre, v2), (64, 128, v2, Pre)):
            copy_op("dve", Vh[lo_:hi_, :, :, 0], s0[lo_:hi_, 3:13, :].rearrange("p k g -> p g k"), P, ["Vh"])
            yield
            copy_op("dve", Vh[lo_:hi_, :, :, 1], s1_[lo_:hi_, 3:13, :].rearrange("p k g -> p g k"), P, ["Vh"])
            yield

    def build_dve(g, main, SL, sl, kp, bsl):
        gj = g % 8
        c0 = gj * 16
        BL, blk_ = BLx[bsl], "BLx%d" % bsl
        o = BL[:, :, c0:c0 + 16]
        tt("dve", t_a[:, 0:8, :], bass.AP(bp1c, g * 16, [[512, 128], [0, 8], [1, 16]]), bass.AP(kpre, g, [[256, 128], [32, 8], [0, 16]]), ALU.mult, P, ["t_a"])
        tt("dve", t_b[:, 0:8, :], bass.AP(bp2c, g * 16, [[512, 128], [0, 8], [1, 16]]), bass.AP(nkpims, g, [[256, 128], [32, 8], [0, 16]]), ALU.mult, P, ["t_b"])
        tt("dve", o, t_a[:, 0:8, :], t_b[:, 0:8, :], ALU.add, ["t_a", "t_b"], [blk_])
        if main:
            copy_op("dve", B0pad[:, gj, c0:c0 + 16], BL[:, 0, c0:c0 + 16], [blk_], ["B0pad"])
            tt("dve", t_a[:, :, :], bass.AP(cpc, g * 16, [[512, 128], [0, 9], [1, 16]]), bass.AP(sAre, g, [[288, 128], [32, 9], [0, 16]]), ALU.mult, P, ["t_a"])
            tt("dve", t_b[:, :, :], bass.AP(cp2c, g * 16, [[512, 128], [0, 9], [1, 16]]), bass.AP(nAim, g, [[288, 128], [32, 9], [0, 16]]), ALU.mult, P, ["t_b"])
            tt("dve", COPs[:, gj, :, c0:c0 + 16], t_a[:, :, :], t_b[:, :, :], ALU.add, ["t_a", "t_b"], ["COPs"])
        mk = "%sMOPc%d" % (kp, sl)
        tt("dve", SL["MOPc"][:, :, :].rearrange("p k (h m) -> p k h m", h=2),
           bass.AP(Eb, 0, [[64, 128], [0, NLC], [0, 2], [1, 64]]),
           bass.AP(Vh, g * NLC * 2, [[32 * NLC * 2, 128], [2, NLC], [1, 2], [0, 64]]), ALU.mult, ["Eb", "Vh"], [mk])

    def build_pe(g, SL, sl, kp, bsl):
        c0 = (g % 8) * 16
        BL, blk_ = BLx[bsl], "BLx%d" % bsl
        sk_ = "%sSOPe%d" % (kp, sl)
        for half in range(2):
            b = PS()
            def fn(e, half=half, b=b):
                last = None
                for k in range(4):
                    last = e.matmul(psum[b][:, k * 128:(k + 1) * 128], lhsT=BL[:, half * 4 + k, :], rhs=ident_bf[:], start=True, stop=True)
                return last
            S.op("pe", fn, [blk_, "ident_bf"], ["ps%d" % b])
            copy_op("act", SL["SOPe"][:, half * 4:half * 4 + 4, :], psum[b][:, :].rearrange("p (a b) -> p a b", a=4), ["ps%d" % b], [sk_])
        S.op("dve", lambda e: e.memset(BL[:, :, c0:c0 + 16], 0.0), [], [blk_])

    def p_build_dve(gs, ob):
        for sl, g in enumerate(gs):
            build_dve(g, False, POPS[ob + sl], ob + sl, "p", sl)

    def p_build_pe(gs, ob):
        for sl, g in enumerate(gs):
            build_pe(g, POPS[ob + sl], ob + sl, "p", sl)

    def p_compute(gs, ob):
        for sl, g in enumerate(gs):
            blk, SL, OP = g // 8, PSL[sl], POPS[ob + sl]
            for cb in (0, 384):
                b = PS()
                mm_group(b, 384, [(OP["SOPe"][:, 7 - i, :], U_pd[:, blk, i, cb:cb + 384]) for i in range(8)], ["pSOPe%d" % (ob + sl), "U_pd"])
                copy_op("act", SL["T0"][:, 256 + cb:256 + cb + 384], psum[b][:, 0:384], ["ps%d" % b], ["T0_%d" % sl])

    def p_levels(gs, ob):
        N, pst = 1024, 1024
        for k in range(10):
            Nh = N // 2
            for sl, g in enumerate(gs):
                SL, OP = PSL[sl], POPS[ob + sl]
                src, skey = (SL["T0"], "T0_%d" % sl) if k == 0 else ((SL["Tb"], "Tb_%d" % sl) if k % 2 == 1 else (SL["Tc"], "Tc_%d" % sl))
                dst, dkey = (SL["Tb"], "Tb_%d" % sl) if k % 2 == 0 else (SL["Tc"], "Tc_%d" % sl)
                b = PS()
                odd = bass.AP(src, 1, [[pst, 128], [2, Nh]])
                even = bass.AP(src, 0, [[pst, 128], [2, Nh]])
                mm_group(b, Nh, [(ident_bf[:], odd), (OP["MOPc"][:, k, :], even)], ["ident_bf", "pMOPc%d" % (ob + sl), skey])
                if k < 9:
                    copy_op("act", dst[:, 0:Nh], psum[b][:, 0:Nh], ["ps%d" % b], [dkey])
                else:
                    copy_op("act", Xin[:, g:g + 1], psum[b][:, 0:1], ["ps%d" % b], ["Xin"])
            N, pst = Nh, 512

    def m_build_dve(gs, ob):
        for sl, g in enumerate(gs):
            build_dve(g, True, MOPS[ob + sl], ob + sl, "m", sl)

    def m_build_pe(gs, ob):
        for sl, g in enumerate(gs):
            build_pe(g, MOPS[ob + sl], ob + sl, "m", sl)

    def m_compute(gs, ob):
        for sl, g in enumerate(gs):
            blk, SL, OP = g // 8, MSL[sl], MOPS[ob + sl]
            b = PS()
            mm_group(b, CW, [(OP["SOPe"][:, 7 - i, :], U_d[:, blk, i, 0:CW]) for i in range(8)], ["mSOPe%d" % (ob + sl), "U"])
            copy_op("act", SL["Xa"][:, PADC + 1:PADC + 1 + CW], psum[b][:, 0:CW], ["ps%d" % b], ["Xa%d" % sl])
            copy_op("act", SL["Xa"][:, PADC:PADC + 1], Xin[:, g:g + 1], ["Xin"], ["Xa%d" % sl])

    def m_levels(gs, ob):
        for k in range(9):
            sh = 1 << k
            for sl, g in enumerate(gs):
                SL, OP, gj = MSL[sl], MOPS[ob + sl], g % 8
                src, sk = (SL["Xa"], "Xa%d" % sl) if k % 2 == 0 else (SL["Xb"], "Xb%d" % sl)
                b = PS()
                mm_group(b, CW + 1, [(ident_bf[:], src[:, PADC:PADC + CW + 1]), (OP["MOPc"][:, k, :], src[:, PADC - sh:PADC - sh + CW + 1])],
                         ["ident_bf", "mMOPc%d" % (ob + sl), sk])
                if k < 8:
                    dst, dk = (SL["Xb"], "Xb%d" % sl) if k % 2 == 0 else (SL["Xa"], "Xa%d" % sl)
                    copy_op("act", dst[:, PADC:PADC + CW + 1], psum[b][:, 0:CW + 1], ["ps%d" % b], [dk])
                else:
                    copy_op("act", Xs_all[:, gj, 0:CW + 1], psum[b][:, 0:CW + 1], ["ps%d" % b], ["Xs"])

    w_in0s = sb("w_in0s", [128, 8, 512], BF16, RA + 41984)
    hT_p = [sb("hT_p0", [128, 8, 512], BF16, RA + 50176), sb("hT_p1", [128, 8, 512], BF16, RA + 58368)]
    xt_p = [sb("xtp%d" % i_, [128, 1024], F32, 118784 + 4096 * i_) for i_ in range(3)]
    h_p = [sb("h_p%d" % i_, [128, 1024], BF16, 131072 + 2048 * i_) for i_ in range(3)]
    load_w(w_in0s, w_in0, 8, 512, "w_in0s", gcol=V_G0, c_lo=512)
    setup_gen = ssm_setup()
    mark('pre_setup')
    for i_ in range(4):
        S.op("dve", lambda e, i_=i_: e.memset(BLx[i_][:, :, :], 0.0), [], ["BLx%d" % i_])
    ptiles = [(sp, t) for sp in range(3) for t in range(16)]
    diag_gen = diag_gen_fn()

    xt_p4 = xt_p + [stg[1][:, 0:1024]]
    xk_p4 = ["xtp0", "xtp1", "xtp2", "stg1"]

    def p_src(i):
        sp, t = ptiles[i]
        xb, xk = xt_p4[i % 4], xk_p4[i % 4]
        return (lambda: dma(xb[:], xpre[sp, t * 128:(t + 1) * 128, :], [], [xk])), xb[:], [xk], h_p[i % 3][:], "h_p%d" % (i % 3)

    junk_p = sb("junk_p", [128, 1024], BF16, 137216)
    p_pro, p_step = make_rms_pipe(p_src, "dve", junk_p)

    def pB(i):
        bi = i // 4
        transpose8(h_p[i % 3], "h_p%d" % (i % 3), hT_p[bi % 2], (i % 4) * 128, "hTp%d_%d" % (bi % 2, i % 4))

    def pC(bi):
        sp, t0 = bi // 4, (bi % 4) * 512
        hTb = hT_p[bi % 2]
        for m in range(4):
            b = PS()
            mm_group(b, 512, [(w_in0s[:, k, m * 128:(m + 1) * 128], hTb[:, k, 0:512]) for k in range(8)],
                     ["w_in0s"] + ["hTp%d_%d" % (bi % 2, q) for q in range(4)])
            cb = sp * 256 + t0 // 8
            copy_op(evac_eng(), U_pd[:, m, :, cb:cb + 64], psum[b][:, 0:512].rearrange("p (c i) -> p i c", i=8), ["ps%d" % b], ["U_pd"])

    p_pro(48)
    for i in range(48):
        pB(i)
        p_step(i, 48)
        if i % 4 == 3:
            pC(i // 4)
        for _ in range(8):
            next(setup_gen, None)
        if i % 4 == 1:
            next(diag_gen, None)
    for _ in setup_gen:
        pass
    for _ in diag_gen:
        pass
    S.barrier()
    for i_ in range(4):
        S.op("dve", lambda e, i_=i_: e.memset(PSL[i_]["T0"][:, 0:256], 0.0), [], ["T0_%d" % i_])
    mark('pre_tiles')
    pbat = [[g0, g0 + 1, g0 + 2, g0 + 3] for g0 in range(0, 32, 4)]
    p_build_dve(pbat[0], 0)
    p_build_pe(pbat[0], 0)
    for k_, gs_ in enumerate(pbat):
        ob_ = (k_ % 2) * 4
        p_compute(gs_, ob_)
        if k_ + 1 < len(pbat):
            p_build_dve(pbat[k_ + 1], ((k_ + 1) % 2) * 4)
        p_levels(gs_, ob_)
        if k_ + 1 < len(pbat):
            p_build_pe(pbat[k_ + 1], ((k_ + 1) % 2) * 4)
    tap("Xin", Xin[:], [128, 32], F32, ["Xin"])
    mark('pre_ssm')
    S.barrier()
    load_w(w_in0_t, w_in0, 8, 2048, "w_in0", gcol=V_G0)
    for g in range(4):
        j = rr["stg"] % 2
        rr["stg"] += 1
        dma(stg[j][:, 0:128], pool_w[g], [], ["stg%d" % j])
        copy_op(evac_eng(), pool_w_t[:, g, :], stg[j][:, 0:128], ["stg%d" % j], ["pool_w"])
    S.op("pool", lambda e: e.memset(upool[:, :, 0:16], 0.0), [], ["upool"])
    xt3 = [xt[0], xt[1], sb("xt2", [128, 1024], F32, RE + 1024)]
    ht3 = [h_t, h_t2, sb("h_t3", [128, 1024], BF16, RE + 5120)]

    pl2 = [pl, sb("pl_b", [128, 512], BF16, RE + 14464)]

    def pool_chain(t0, n, g):
        if True:
            plb, plk = pl2[g % 2], "pl%d" % (g % 2)
            w = 2 << g
            u = lambda lo, hi, g=g: upool[:, g, t0 + lo:t0 + hi]
            N = 16 + n
            if g == 0:
                tt("dve", wb[:, 16:N], u(16, N), u(15, N - 1), ALU.add, ["upool"], ["wb"])
            elif g == 1:
                tt("dve", wa[:, 14:N], u(14, N), u(13, N - 1), ALU.add, ["upool"], ["wa"])
                tt("dve", wb[:, 16:N], wa[:, 16:N], wa[:, 14:N - 2], ALU.add, ["wa"], ["wb"])
            elif g == 2:
                tt("dve", wa[:, 10:N], u(10, N), u(9, N - 1), ALU.add, ["upool"], ["wa"])
                tt("dve", wb[:, 12:N], wa[:, 12:N], wa[:, 10:N - 2], ALU.add, ["wa"], ["wb"])
                tt("dve", wa[:, 16:N], wb[:, 16:N], wb[:, 12:N - 4], ALU.add, ["wb"], ["wa"])
                copy_op("dve", wb[:, 16:N], wa[:, 16:N], ["wa"], ["wb"])
            else:
                tt("dve", wa[:, 2:N], u(2, N), u(1, N - 1), ALU.add, ["upool"], ["wa"])
                tt("dve", wb[:, 4:N], wa[:, 4:N], wa[:, 2:N - 2], ALU.add, ["wa"], ["wb"])
                tt("dve", wa[:, 8:N], wb[:, 8:N], wb[:, 4:N - 4], ALU.add, ["wb"], ["wa"])
                tt("dve", wb[:, 16:N], wa[:, 16:N], wa[:, 8:N - 8], ALU.add, ["wa"], ["wb"])
            stt("dve", plb[:, 0:n], wb[:, 16:N], 1.0 / w, u(16, N), ALU.mult, ALU.subtract, ["wb", "upool"], [plk])
            if t0 == 0:
                tt("dve", ptmp[:, :], wb[:, 16 + 128:16 + 144], vecs[:, V_INV + g * 16:V_INV + g * 16 + 16], ALU.mult, ["wb", "vecs"], ["ptmp"])
                tt("dve", plb[:, 128:144], ptmp[:, :], u(16 + 128, 16 + 144), ALU.subtract, ["ptmp", "upool", plk], [plk])

    def pool_mix(t0, n, g):
        plb, plk = pl2[g % 2], "pl%d" % (g % 2)
        b = PS()
        mm_group(b, n, [(pool_w_t[:, g, :], plb[:, 0:n])], ["pool_w", plk])
        stt("dve", yg[:, g, t0:t0 + n], psum[b][:, 0:n], vecs[:, V_PSC + g:V_PSC + g + 1], s_t[:, g, t0:t0 + n],
            ALU.mult, ALU.mult, ["ps%d" % b, "vecs", "s"], ["yg"])

    def pool_sched(blk, m):
        t0, n = blk
        if m == 4:
            pool_chain(t0, n, 0)
            pool_chain(t0, n, 1)
        elif m == 8:
            pool_mix(t0, n, 0)
            pool_mix(t0, n, 1)
            pool_chain(t0, n, 2)
            pool_chain(t0, n, 3)
        elif m == 12:
            pool_mix(t0, n, 2)
            pool_mix(t0, n, 3)

    xt4b = xt3 + [stg[1][:, 0:1024]]
    xk4b = ["xt0", "xt1", "xt2", "stg1"]

    def m_src(i):
        xb, xk = xt4b[i % 4], xk4b[i % 4]
        return (lambda: dma(xb[:], xin[i * 128:(i + 1) * 128, :], [], [xk])), xb[:], [xk], ht3[i % 3][:], "h%d" % (i % 3)

    junk_m = sb("junk_m", [128, 1024], BF16, RE + 7168)
    m_pro, m_step = make_rms_pipe(m_src, "dve", junk_m)

    def mB(i):
        transpose8(ht3[i % 3], "h%d" % (i % 3), hT, (i % 4) * 128, "hT_%d" % (i % 4))

    m_pro(NT)
    prev_blk = None
    for (t0, n) in token_blocks(512):
        for ti in range(n // 128):
            tile_i = t0 // 128 + ti
            mB(tile_i)
            m_step(tile_i, NT)
        for m in range(16):
            if prev_blk is not None:
                pool_sched(prev_blk, m)
            b = PS()
            mm_group(b, n, [(w_in0_t[:, k, m * 128:(m + 1) * 128], hT[:, k, 0:n]) for k in range(8)], ["w_in0"] + ["hT_%d" % q for q in range(n // 128)])
            pk = "ps%d" % b
            if m < 4:
                copy_op("act", upool[:, m, 16 + t0:16 + t0 + n], psum[b][:, 0:n], [pk], ["upool"])
            elif m < 8:
                copy_op("dve", U_d[:, m - 4, :, t0 // 8:(t0 + n) // 8], psum[b][:, 0:n].rearrange("p (c i) -> p i c", i=8), [pk], ["U"])
            else:
                act(s_t[:, m - 8, t0:t0 + n], psum[b][:, 0:n], AF.Silu, [pk], ["s"])
        prev_blk = (t0, n)
    for m_ in (4, 8, 12):
        pool_sched(prev_blk, m_)
    tap("ygp", yg[:, 0, :], [128, T], BF16, ["yg"])
    tap("s", s_t[:, 0, :], [128, T], BF16, ["s"])
    S.barrier()

    for i_ in range(3):
        S.op("dve", lambda e, i_=i_: e.memset(BLx[i_][:, :, :], 0.0), [], ["BLx%d" % i_])
    S.op("dve", lambda e: e.memset(B0pad[:, :, :], 0.0), [], ["B0pad"])
    S.op("dve", lambda e: e.memset(COPs[:, :, :, :], 0.0), [], ["COPs"])
    for i_ in range(3):
        S.op("dve", lambda e, i_=i_: e.memset(MSL[i_]["Xa"][:, 0:PADC], 0.0), [], ["Xa%d" % i_])
        S.op("dve", lambda e, i_=i_: e.memset(MSL[i_]["Xb"][:, 0:PADC], 0.0), [], ["Xb%d" % i_])
    for blk in range(4):
        mbat = [[blk * 8 + 0, blk * 8 + 1, blk * 8 + 2], [blk * 8 + 3, blk * 8 + 4, blk * 8 + 5], [blk * 8 + 6, blk * 8 + 7]]
        m_build_dve(mbat[0], 0)
        m_build_pe(mbat[0], 0)
        for k_, gs_ in enumerate(mbat):
            ob_ = (k_ % 2) * 3
            m_compute(gs_, ob_)
            if k_ + 1 < 3:
                m_build_dve(mbat[k_ + 1], ((k_ + 1) % 2) * 3)
            m_levels(gs_, ob_)
            if k_ + 1 < 3:
                m_build_pe(mbat[k_ + 1], ((k_ + 1) % 2) * 3)
        for half in range(2):
            b = PS()
            def fnk(e, half=half, b=b):
                last = None
                for tq in range(4):
                    for gj in range(8):
                        last = e.matmul(psum[b][:, tq * 128:(tq + 1) * 128], lhsT=B0pad[:, gj, :], rhs=COPs[:, gj, half * 4 + tq, :],
                                        start=(gj == 0), stop=(gj == 7))
                return last
            S.op("pe", fnk, ["B0pad", "COPs"], ["ps%d" % b])
            copy_op(evac_eng(), KOP[:, half * 4:half * 4 + 4, :], psum[b][:, :].rearrange("p (a b) -> p a b", a=4), ["ps%d" % b], ["KOP"])
        ysv = ysm[:, blk, :].rearrange("p (c i) -> p i c", i=8)
        for j in range(8):
            b = PS()
            pairs = [(KOP[:, j - i, :], U_d[:, blk, i, 0:CW]) for i in range(j + 1)]
            pairs += [(COPs[:, gj, j + 1, :], Xs_all[:, gj, 0:CW]) for gj in range(8)]
            mm_group(b, CW, pairs, ["KOP", "U", "COPs", "Xs"])
            stt("dve", ysv[:, j, :], U_d[:, blk, j, 0:CW], vecs[:, V_SSD + blk:V_SSD + blk + 1], psum[b][:, 0:CW],
                ALU.mult, ALU.add, ["U", "vecs", "ps%d" % b], ["ysm"])
    tap("ysm", ysm[:, 0, :], [128, T], BF16, ["ysm"])
    S.barrier()

    load_w(w_glu_t, w_glu, 4, 1024, "w_glu", half=512)
    load_w(w_out0_t, w_out0, 8, 1024, "w_out0")
    for (t0, n) in TB:
        for m in range(4):
            bv, bg = PS(), PS()
            mm_group(bv, n, [(w_glu_t[:, k, m * 128:(m + 1) * 128], ysm[:, k, t0:t0 + n]) for k in range(4)], ["w_glu", "ysm"])
            mm_group(bg, n, [(w_glu_t[:, k, 512 + m * 128:512 + (m + 1) * 128], ysm[:, k, t0:t0 + n]) for k in range(4)], ["w_glu", "ysm"])
            sg = sgt[m % 2]
            sgk = "sgt%d" % (m % 2)
            act(sg[:, 0:n], psum[bg][:, 0:n], AF.Tanh, ["ps%d" % bg], [sgk], scale=0.5)
            stt("dve", sg[:, 0:n], sg[:, 0:n], 1.0, psum[bv][:, 0:n], ALU.add, ALU.mult, ["ps%d" % bv, sgk], [sgk])
            tt("dve", yg[:, 4 + m, t0:t0 + n], sg[:, 0:n], s_t[:, 4 + m, t0:t0 + n], ALU.mult, [sgk, "s"], ["yg"])
    S.barrier()

    load_w(w_in1_t, w_in1, 5, 3072, "w_in1", gcol=V_G1, half=1024)
    for ti in range(NT):
        xb = xt4[ti % 2]
        xk = "xt4%d" % (ti % 2)
        dma(xb[:], xin[ti * 128:(ti + 1) * 128, :], [], [xk])
        for half in range(2):
            b = PS()
            mm_group(b, 512, [(yg[:, k, ti * 128:(ti + 1) * 128], w_out0_t[:, k, half * 512:(half + 1) * 512]) for k in range(8)], ["yg", "w_out0"])
            tt("dve", x1[:, ti, half * 512:(half + 1) * 512], psum[b][:, :], xb[:, half * 512:(half + 1) * 512], ALU.add,
               ["ps%d" % b, xk], ["x1"])
    tap("x1", x1[:, 1, :], [128, 1024], F32, ["x1"])
    S.barrier()

    dma(gainf[:], gains[2], [], ["gainf"])
    rr["stg"] = 0
    stg1 = [sb("stgL0", [128, 2048], F32, RF), sb("stgL1", [128, 2048], F32, RF + 8192)]
    load_w(w_in1_t, w_in1, 8, 3072, "w_in1", gcol=V_G1, k_lo=5, half=1024)
    load_w(w_out1_t, w_out1, 8, 1024, "w_out1")
    S.barrier()
    S.op("pool", lambda e: e.memset(g_t[:, :, 0:30], 0.0), [], ["g"])
    h13 = [h1, h1b, sb("h1c", [128, 1024], BF16, RH + 4096)]

    def l_src(i):
        return None, x1[:, i, :], ["x1"], h13[i % 3][:], "h1_%d" % (i % 3)

    l_pro, l_step = make_rms_pipe(l_src, "act", sq1)

    def lB(i):
        transpose8(h13[i % 3], "h1_%d" % (i % 3), hT1, (i % 2) * 128, "hT1_%d" % (i % 2))

    LB = token_blocks(256)
    ct2s = [ct2, sb("ct2b", [128, 256], F32, 172288)]

    def hkeys(n):
        return ["hT1_%d" % q for q in range(n // 128)]

    def prep(bi):
        t0, n = LB[bi]
        for ti in range(n // 128):
            tile_i = t0 // 128 + ti
            lB(tile_i)
            l_step(tile_i, NT)

    def in_vg(bi, m):
        t0, n = LB[bi]
        bv, bg = PS(), PS()
        mm_group(bv, n, [(w_in1_t[:, k, m * 128:(m + 1) * 128], hT1[:, k, 0:n]) for k in range(8)], ["w_in1"] + hkeys(n))
        mm_group(bg, n, [(w_in1_t[:, k, 1024 + m * 128:1024 + (m + 1) * 128], hT1[:, k, 0:n]) for k in range(8)], ["w_in1"] + hkeys(n))
        sg = sg1[m % 2]
        sgk = "sg1%d" % (m % 2)
        act(sg[:, 0:n], psum[bg][:, 0:n], AF.Tanh, ["ps%d" % bg], [sgk], scale=0.5)
        stt("dve", g_t[:, m, 30:30 + n], sg[:, 0:n], 1.0, psum[bv][:, 0:n], ALU.add, ALU.mult, ["ps%d" % bv, sgk], ["g"])

    def in_z(bi, m):
        t0, n = LB[bi]
        b = PS()
        mm_group(b, n, [(w_in1_t[:, k, 2048 + m * 128:2048 + (m + 1) * 128], hT1[:, k, 0:n]) for k in range(8)], ["w_in1"] + hkeys(n))
        act(s1[:, m, 0:n], psum[b][:, 0:n], AF.Silu, ["ps%d" % b], ["s1_%d" % m])

    pend_stores = []

    def flush_stores():
        for f_ in pend_stores:
            f_()
        del pend_stores[:]

    def dg_load(m):
        dma(diag2[m % 2][:, :, :].rearrange("p a b -> p (a b)"), dgs[m][:, TD * 128:31 * 128], ["dgs"], ["diag%d" % (m % 2)])

    ot2 = [ot, sb("otb", [128, 1024], F32, 155904)]

    def conv(bi):
        t0, n = LB[bi]
        bs_, bq_ = PS(), PS()
        pend = None
        for m in range(8):
            dg, dgk = diag2[m % 2], "diag%d" % (m % 2)
            b = PS()
            while b in (bs_, bq_):
                b = PS()
            mm_group(b, n, [(dg[:, k - TD, :], g_t[:, m, k:k + n]) for k in range(TD, 31)], [dgk, "g"])
            if m + 2 < 8:
                dg_load(m + 2)
            ca, cak = cacc[m % 2], "cacc%d" % (m % 2)
            ts("dve", ca[:, 0:n], g_t[:, m, 0:n], vecs[:, V_CW + m * 31:V_CW + m * 31 + 1], None, ALU.mult, None, ["g", "vecs"], [cak])
            for k in range(1, TD):
                stt("dve", ca[:, 0:n], g_t[:, m, k:k + n], vecs[:, V_CW + m * 31 + k:V_CW + m * 31 + k + 1], ca[:, 0:n], ALU.mult, ALU.add,
                    ["g", "vecs", cak], [cak])
            cbc = vecs[:, V_CB + m:V_CB + m + 1]
            stt("dve", c_t[:, m, 0:n], psum[b][:, 0:n], cbc, ca[:, 0:n], ALU.add, ALU.add, ["ps%d" % b, "vecs", cak], ["c%d" % m])
            cb_, cq_ = cbf[m % 2], csq[m % 2]
            cbk, cqk = "cbf%d" % (m % 2), "csq%d" % (m % 2)
            copy_op("act", cb_[:, 0:n], c_t[:, m, 0:n], ["c%d" % m], [cbk])
            act(cq_[:, 0:n], c_t[:, m, 0:n], AF.Square, ["c%d" % m], [cqk])
            def stats(m=m, cb_=cb_, cq_=cq_, cbk=cbk, cqk=cqk):
                def fn1(e):
                    return e.matmul(psum[bs_][:, 0:n], lhsT=ones_bf[:], rhs=cb_[:, 0:n], start=(m == 0), stop=(m == 7))
                S.op("pe", fn1, ["ones_bf", cbk], ["ps%d" % bs_])
                def fn2(e):
                    return e.matmul(psum[bq_][:, 0:n], lhsT=ones_bf[:], rhs=cq_[:, 0:n], start=(m == 0), stop=(m == 7))
                S.op("pe", fn2, ["ones_bf", cqk], ["ps%d" % bq_])
            if pend is not None:
                pend()
            pend = stats
        pend()
        copy_op("dve", g_t[:, :, 0:30], g_t[:, :, n:n + 30], ["g"], ["g"])
        flush_stores()
        return bs_, bq_

    def ln_head(bi, bs_, bq_):
        t0, n = LB[bi]
        ts("dve", mean[:, 0:n], psum[bs_][:, 0:n], 1.0 / 1024, None, ALU.mult, None, ["ps%d" % bs_], ["mean"])
        tt("dve", ct[:, 0:n], mean[:, 0:n], mean[:, 0:n], ALU.mult, ["mean"], ["ct"])
        stt("dve", rstd[:, 0:n], psum[bq_][:, 0:n], 1.0 / 1024, ct[:, 0:n], ALU.mult, ALU.subtract, ["ps%d" % bq_, "ct"], ["rstd"])
        ts("dve", rstd[:, 0:n], rstd[:, 0:n], 1e-5, None, ALU.add, None, ["rstd"], ["rstd"])
        act(rstd[:, 0:n], rstd[:, 0:n], AF.Sqrt, ["rstd"], ["rstd"])
        S.op("dve", lambda e, n=n: e.reciprocal(out=rstd[:, 0:n], in_=rstd[:, 0:n]), ["rstd"], ["rstd"])

    def norm_m(bi, m):
        t0, n = LB[bi]
        c2, c2k = ct2s[m % 2], "ct2_%d" % (m % 2)
        tt("dve", ct[:, 0:n], c_t[:, m, 0:n], mean[:, 0:n], ALU.subtract, ["c%d" % m, "mean"], ["ct"])
        tt("dve", c2[:, 0:n], ct[:, 0:n], rstd[:, 0:n], ALU.mult, ["ct", "rstd"], [c2k])
        act(c2[:, 0:n], c2[:, 0:n], AF.Silu, [c2k, "vecs"], [c2k], bias=vecs[:, V_LB + m:V_LB + m + 1], scale=vecs[:, V_LG + m:V_LG + m + 1])
        tt("dve", y1[:, m, 0:n], c2[:, 0:n], s1[:, m, 0:n], ALU.mult, [c2k, "s1_%d" % m], ["y1"])

    def outp(bi):
        t0, n = LB[bi]
        for ti in range(n // 128):
            tile_i = t0 // 128 + ti
            otb, otk = ot2[tile_i % 2], "ot%d" % (tile_i % 2)
            for half in range(2):
                b = PS()
                mm_group(b, 512, [(y1[:, k, ti * 128:(ti + 1) * 128], w_out1_t[:, k, half * 512:(half + 1) * 512]) for k in range(8)], ["y1", "w_out1"])
                tt("dve", otb[:, half * 512:(half + 1) * 512], psum[b][:, :], x1[:, tile_i, half * 512:(half + 1) * 512], ALU.add,
                   ["ps%d" % b, "x1"], [otk])
            if tile_i >= 1:
                final_norm(otb[:], [otk], gainf, "gainf", sq1)
                pend_stores.append(lambda tile_i=tile_i, otb=otb, otk=otk: dma(out_d[(tile_i - 1) * 128:tile_i * 128, :], otb[:], [otk], ["out"]))

    l_pro(NT)
    prep(0)
    for m in range(8):
        in_vg(0, m)
        in_z(0, m)
    dg_load(0)
    dg_load(1)
    for bi in range(len(LB)):
        bs_, bq_ = conv(bi)
        nxt = bi + 1 < len(LB)
        if nxt:
            prep(bi + 1)
        ln_head(bi, bs_, bq_)
        for m in range(8):
            if nxt:
                in_vg(bi + 1, m)
            if m < 4:
                norm_m(bi, 2 * m)
                norm_m(bi, 2 * m + 1)
            if nxt:
                in_z(bi + 1, m)
            if m == 4:
                if nxt:
                    dg_load(0)
                    dg_load(1)
                outp(bi)
    flush_stores()
    S.barrier()

    with nc.Block() as block:
        @block.sync
        def _(e):
            S.replay("sp", e)

        @block.scalar
        def _(e):
            S.replay("act", e)

        @block.vector
        def _(e):
            S.replay("dve", e)

        @block.gpsimd
        def _(e):
            S.replay("pool", e)

        @block.tensor
        def _(e):
            S.replay("pe", e)
    st.close()
    return nc, taps


def _prep_inputs(inp):
    f = lambda a: np.ascontiguousarray(np.asarray(a, dtype=np.float32))
    x = f(inp["x"])
    gains = np.stack([np.broadcast_to(f(inp["even_norm"])[0], (128, 1024)),
                      np.broadcast_to(f(inp["odd_norm"])[0], (128, 1024)),
                      np.broadcast_to(f(inp["final_norm"]), (128, 1024))]).copy()
    a_re, a_im, ldt = f(inp["ssm_a_re"])[0], f(inp["ssm_a_im"])[0], f(inp["ssm_log_dt"])[0]
    ssmA = np.zeros((128, 3, 32), np.float32)
    ssmA[:, 0, :] = np.concatenate([a_re.T, a_re.T], 0)
    ssmA[:, 1, :] = np.concatenate([a_im.T, a_im.T], 0)
    ssmA[:, 2, :] = ldt[None, :]
    b_re, b_im = f(inp["ssm_b_re"])[0], f(inp["ssm_b_im"])[0]
    c_re, c_im = f(inp["ssm_c_re"])[0], f(inp["ssm_c_im"])[0]
    bp1 = np.zeros((128, 32, 128), np.float32)
    bp2 = np.zeros((128, 32, 128), np.float32)
    cp = np.zeros((128, 32, 128), np.float32)
    for g in range(32):
        c0 = (g % 8) * 16
        bp1[0:64, g, c0:c0 + 16] = b_re[g]
        bp1[64:128, g, c0:c0 + 16] = b_im[g]
        bp2[0:64, g, c0:c0 + 16] = b_im[g]
        bp2[64:128, g, c0:c0 + 16] = b_re[g]
        cp[0:64, g, c0:c0 + 16] = c_re[g].T
        cp[64:128, g, c0:c0 + 16] = c_im[g].T
    bp1c = np.zeros((128, 32, 16), np.float32)
    bp2c = np.zeros((128, 32, 16), np.float32)
    cpc = np.zeros((128, 32, 16), np.float32)
    cp2c = np.zeros((128, 32, 16), np.float32)
    for g in range(32):
        bp1c[0:64, g], bp1c[64:128, g] = b_re[g], b_im[g]
        bp2c[0:64, g], bp2c[64:128, g] = b_im[g], b_re[g]
        cpc[0:64, g], cpc[64:128, g] = c_re[g].T, c_im[g].T
        cp2c[0:64, g], cp2c[64:128, g] = c_im[g].T, c_re[g].T
    ident = np.eye(128, dtype=np.float32)
    jswap = np.zeros((128, 128), np.float32)
    for k in range(128):
        jswap[k, (k + 64) % 128] = 1.0
    vbase = np.zeros((128, NV), np.float32)
    vbase[:, V_PSC:V_PSC + 4] = f(inp["pool_scale"])[0].reshape(4, 128).T
    vbase[:, V_SSD:V_SSD + 4] = f(inp["ssm_d"])[0].reshape(4, 128).T
    vbase[:, V_CB:V_CB + 8] = f(inp["conv_b"])[0].reshape(8, 128).T
    vbase[:, V_LG:V_LG + 8] = f(inp["conv_ln_g"])[0].reshape(8, 128).T
    vbase[:, V_LB:V_LB + 8] = f(inp["conv_ln_b"])[0].reshape(8, 128).T
    vbase[:, V_CW:V_CW + 248] = f(inp["conv_w"])[0].reshape(31, 8, 128).transpose(2, 1, 0).reshape(128, 248)
    vbase[:, V_G0:V_G0 + 8] = f(inp["even_norm"])[0].reshape(8, 128).T
    vbase[:, V_G1:V_G1 + 8] = f(inp["odd_norm"])[0].reshape(8, 128).T
    vbase[0:64, V_SG] = 1.0
    vbase[64:128, V_SG] = -1.0
    common = {
        "w_in0": f(inp["even_w_in"])[0], "w_glu": f(inp["ssm_w_glu"])[0], "w_out0": f(inp["even_w_out"])[0],
        "pool_w": f(inp["pool_w"])[0], "w_in1": f(inp["odd_w_in"])[0], "w_out1": f(inp["odd_w_out"])[0],
        "gains": gains, "ssmA": ssmA, "bp1": bp1, "bp2": bp2, "cp": cp, "bp1c": bp1c, "bp2c": bp2c, "cpc": cpc, "cp2c": cp2c, "ident": ident, "jswap": jswap,
    }
    maps = []
    for c in range(8):
        b, q = c // 4, c % 4
        t0 = q * 2048
        xin = np.zeros((T, 1024), np.float32)
        if q == 0:
            xin[128:] = x[b, 0:2048]
        else:
            xin[:] = x[b, t0 - 128:t0 + 2048]
        v = vbase.copy()
        for g in range(4):
            w = 2 << g
            for i in range(16):
                v[:, V_INV + g * 16 + i] = (1.0 / min(i + 1, w)) if q == 0 else (1.0 / w)
        for qq in range(q):
            v[:, V_SEL + (b * 4 + qq) * 3 + (q - 1 - qq)] = 1.0
        m = dict(common)
        xpre = np.zeros((3, 2048, 1024), np.float32)
        for sp in range(3):
            lo = t0 - 128 - (3 - sp) * 2048
            hi = lo + 2048
            if hi > 0:
                l2 = max(lo, 0)
                xpre[sp, l2 - lo:] = x[b, l2:hi]
        m["xpre"] = xpre
        m["xin"] = xin
        m["vecs"] = v
        maps.append(m)
    return maps


_NC_CACHE = {}


def kernel(**inputs):
    maps = _prep_inputs(inputs)
    if "nc" not in _NC_CACHE:
        _NC_CACHE["nc"] = build_nc()
    nc, taps = _NC_CACHE["nc"]
    res = run_bass_kernel_spmd(nc, maps, core_ids=list(range(8)))
    out = np.zeros((2, 8192, 1024), np.float32)
    for c in range(8):
        b, q = c // 4, c % 4
        out[b, q * 2048:(q + 1) * 2048] = res.results[c]["out"]
    if DEBUG:
        kernel.last = res
    return out
```

```python
import math
import numpy as np
import concourse.bass as bass
import concourse.mybir as mybir
from concourse.bass_utils import run_bass_kernel_spmd

F32 = mybir.dt.float32
BF16 = mybir.dt.bfloat16
AF = mybir.ActivationFunctionType
ALU = mybir.AluOpType

NT = 17
T = NT * 128
PAD = 2048
NLEV = 12
TWO_PI = 2.0 * math.pi
DEBUG = False
_MARKS = []
USE_CC = False


class Sched:
    ENG = ("pe", "act", "dve", "pool", "sp")
    NDMA = 12

    def __init__(self, sems, dma_sems):
        self.sem = dict(zip(self.ENG, sems))
        self.dma_sems = dma_sems
        self.lists = {e: [] for e in self.ENG}
        self.count = {e: 0 for e in self.ENG}
        self.ndma = 0
        self.last_w = {}
        self.readers = {}
        self.seen = {e: {} for e in self.ENG}
        self.last_ev = {}
        self.dma_last = {}

    def _need(self, eng, ev, waits):
        if ev is None:
            return
        key, sem, val, peng = ev
        if peng == eng and eng == "pe":
            return
        if self.seen[eng].get(key, 0) >= val:
            return
        cur = waits.get(key)
        if cur is None or cur[1] < val:
            waits[key] = (sem, val)

    def op(self, eng, fn, reads=(), writes=(), dma=False):
        waits = {}
        for k in reads:
            self._need(eng, self.last_w.get(k), waits)
        for k in writes:
            self._need(eng, self.last_w.get(k), waits)
            for ev in self.readers.get(k, ()):
                self._need(eng, ev, waits)
        if dma:
            j = self.ndma
            self.ndma += 1
            slot = j % self.NDMA
            s = self.dma_sems[slot]
            val = 16 * (j // self.NDMA + 1)
            key = ("dma", slot)
            if val > 16:
                self._need(eng, (key, s, val - 16, "dmaq"), waits)
            ev = (key, s, val, "dmaq")
            inc = (s, 16)
            self.dma_last[slot] = ev
        else:
            self.count[eng] += 1
            ev = (("eng", eng), self.sem[eng], self.count[eng], eng)
            inc = (self.sem[eng], 1)
            self.last_ev[eng] = ev
        for key, (sem, val) in waits.items():
            self.lists[eng].append(("wait", sem, val))
            self.seen[eng][key] = val
        self.lists[eng].append(("op", fn, inc))
        for k in reads:
            self.readers.setdefault(k, []).append(ev)
        for k in writes:
            self.last_w[k] = ev
            self.readers[k] = []
        return ev

    def barrier(self):
        evs = list(self.last_ev.values()) + list(self.dma_last.values())
        for eng in self.ENG:
            waits = {}
            for ev in evs:
                self._need(eng, ev, waits)
            for key, (sem, val) in waits.items():
                self.lists[eng].append(("wait", sem, val))
                self.seen[eng][key] = val

    def replay(self, eng, e):
        for item in self.lists[eng]:
            if item[0] == "wait":
                e.wait_ge(item[1], item[2])
            else:
                inst = item[1](e)
                inst.then_inc(item[2][0], item[2][1])


def token_blocks(bs):
    out = []
    t = 0
    while t < T:
        n = min(bs, T - t)
        out.append((t, n))
        t += n
    return out


V_PSC, V_SSD, V_CB, V_LG, V_LB, V_CW, V_INV, V_RM, V_SG, V_SEL = 0, 4, 8, 16, 24, 32, 280, 344, 352, 353
V_G0, V_G1 = 377, 385
NV = 393


def build_nc():
    from contextlib import ExitStack
    nc = bass.Bass("TRN2", target_bir_lowering=False)
    st = ExitStack()
    dram = lambda n, s, k="ExternalInput", d=F32: nc.dram_tensor(n, s, d, kind=k).ap()
    xin = dram("xin", [T, 1024])
    xpre = dram("xpre", [3, 2048, 1024])
    w_in0 = dram("w_in0", [1024, 2048])
    w_glu = dram("w_glu", [512, 1024])
    w_out0 = dram("w_out0", [1024, 1024])
    pool_w = dram("pool_w", [4, 128, 128])
    w_in1 = dram("w_in1", [1024, 3072])
    w_out1 = dram("w_out1", [1024, 1024])
    gains = dram("gains", [3, 128, 1024])
    vecs_d = dram("vecs", [128, NV])
    ssmA_d = dram("ssmA", [128, 3, 32])
    bp1_d = dram("bp1", [128, 32, 128])
    bp2_d = dram("bp2", [128, 32, 128])
    cp_d = dram("cp", [128, 32, 128])
    bp1c_d = dram("bp1c", [128, 32, 16])
    bp2c_d = dram("bp2c", [128, 32, 16])
    cpc_d = dram("cpc", [128, 32, 16])
    cp2c_d = dram("cp2c", [128, 32, 16])
    ident_d = dram("ident", [128, 128])
    jswap_d = dram("jswap", [128, 128])
    out_d = dram("out", [2048, 1024], "ExternalOutput")
    ebuf = nc.dram_tensor("ebuf", [128, 32], F32).ap()
    egat = nc.dram_tensor("egat", [8 * 128, 32], F32).ap()
    taps = {}

    def sb(name, shape, dt, off):
        return nc.alloc_sbuf_tensor_at(name, shape, dt, offset=16576 + off)

    RA, RB, RC, RD, RE, RF, RG, RH = 0, 69632, 104448, 139264, 156672, 174080, 190464, 196608
    s_t = sb("s_t", [128, 8, T], BF16, RB)
    yg = sb("yg", [128, 8, T], BF16, RC)
    U = sb("U", [128, 4, T], BF16, RD)
    ysm = sb("ysm", [128, 4, T], BF16, RE)
    stg = [sb("stg0", [128, 2048], F32, RF), sb("stg1", [128, 2048], F32, RF + 8192)]
    ident_bf = sb("ident_bf", [128, 128], BF16, RG)
    ident_f = sb("ident_f", [128, 128], F32, RG + 256)
    jswap_f = sb("jswap_f", [128, 128], F32, RG + 768)
    ones_bf = sb("ones_bf", [128, 128], BF16, RG + 1280)
    vecs = sb("vecs_t", [128, NV], F32, RG + 1536)
    ss = sb("ss", [128, 8], F32, RG + 3136)
    rs = sb("rs", [128, 8], F32, RG + 3168)
    rt = sb("rt", [128, 8], F32, RG + 3328)
    w_in0_t = sb("w_in0_t", [128, 8, 2048], BF16, RA)
    upool = sb("upool", [128, 4, 16 + T], BF16, RA + 32768)
    hT = sb("hT", [128, 8, 512], BF16, RA + 50304)
    xt = [sb("xt0", [128, 1024], F32, RA + 58496), sb("xt1", [128, 1024], F32, RA + 62592)]
    h_t = sb("h_t", [128, 1024], BF16, RA + 66688)
    h_t2 = sb("h_t2", [128, 1024], BF16, RE + 12416)
    pool_w_t = sb("pool_w_t", [128, 4, 128], BF16, RE)
    gain_t = sb("gain_t", [128, 1024], F32, RE + 1024)
    sq = sb("sq", [128, 1024], F32, RE + 5120)
    wa = sb("wa", [128, 528], BF16, RE + 9216)
    wb = sb("wb", [128, 528], BF16, RE + 10272)
    pl = sb("pl", [128, 512], BF16, RE + 11328)
    ptmp = sb("ptmp", [128, 16], F32, RE + 12352)
    X = [sb("X0", [128, PAD + T], BF16, RA), sb("X1", [128, PAD + T], BF16, RA + 8448)]
    Z = sb("Z", [128, 2304], BF16, RA + 16896)
    bopp = [sb("bopp0", [128, 128], BF16, RA + 21504), sb("bopp1", [128, 128], BF16, RA + 21760)]
    btmp = sb("btmp", [128, 128], F32, RA + 22016)
    bl = sb("bl", [128, 128], BF16, RA + 22528)
    mtmp = sb("mtmp", [128, 128], F32, RA + 22784)
    mtmp2 = sb("mtmp2", [128, 128], F32, RH + 11264)
    mt4 = sb("mt4", [128, NLEV, 128], BF16, RH + 12288)
    PRM = RA + 32768
    prm = lambda name, i, n=32: sb(name, [128, n], F32, PRM + 128 * i)
    a_re, a_im, ldt = prm("a_re", 0), prm("a_im", 1), prm("ldt", 2)
    dt_t, t0_, t1_, t2_, t3_ = prm("dt_t", 3), prm("t0_", 4), prm("t1_", 5), prm("t2_", 6), prm("t3_", 7)
    k_re, nkims = prm("k_re", 8), prm("nkims", 9)
    Eloc = prm("Eloc", 10)
    Xin = sb("Xin", [128, 32], F32, RG + 3200)
    Vh = sb("Vh", [128, 32, 10, 2], F32, RG + 3392)
    Eb = sb("Eb", [128, 64], BF16, RG + 5952)
    qi = sb("qi", [128, 32], mybir.dt.int32, PRM + 6784)
    qf = sb("qf", [128, 32], F32, PRM + 6912)
    c2ims, c1ims = prm("c2ims", 12), prm("c1ims", 13)
    Pre = sb("Pre", [128, 13, 32], F32, PRM + 128 * 14)
    Pim = sb("Pim", [128, 13, 32], F32, PRM + 128 * 14 + 1664)
    v2 = sb("v2", [128, 13, 32], F32, PRM + 128 * 14 + 3328)
    EG = sb("EG", [128, 8, 32], F32, PRM + 6784)
    EGs = sb("EGs", [128, 8, 32], F32, PRM + 7808)
    Yacc = sb("Yacc", [128, 4, T], F32, RA + 32768)
    COP = sb("COP", [128, 32, 128], BF16, RH)
    MOP = sb("MOP", [128, NLEV, 128], BF16, RH + 8192)
    w_glu_t = sb("w_glu_t", [128, 4, 1024], BF16, RA)
    sgt = [sb("sgt0", [128, 512], F32, RA + 8192), sb("sgt1", [128, 512], F32, RA + 10240)]
    w_out0_t = sb("w_out0_t", [128, 8, 1024], BF16, RD)
    xt4 = [sb("xt40", [128, 1024], F32, RE), sb("xt41", [128, 1024], F32, RE + 4096)]
    x1 = sb("x1", [128, NT, 1024], F32, RA)
    w_in1_t = sb("w_in1_t", [128, 8, 3072], BF16, RB)
    w_out1_t = sb("w_out1_t", [128, 8, 1024], BF16, 118784)
    TD = 5
    diag2 = [sb("diag0", [128, 31 - TD, 128], BF16, 135168), sb("diag1", [128, 31 - TD, 128], BF16, RH + 8192)]
    cacc = [sb("cacc0", [128, 256], F32, 141824), sb("cacc1", [128, 256], F32, RH + 14848)]
    hT1 = sb("hT1", [128, 8, 256], BF16, 143104)
    g_t = sb("g_t", [128, 8, 288], BF16, 147200)
    s1 = sb("s1", [128, 8, 256], BF16, 151808)
    gain1 = sb("gain1", [128, 1024], F32, 155904)
    gainf = sb("gainf", [128, 1024], F32, 160000)
    h1 = sb("h1", [128, 1024], BF16, 164096)
    mean = sb("mean", [128, 256], F32, 166144)
    rstd = sb("rstd", [128, 256], F32, 167168)
    sg1 = [sb("sg10", [128, 256], F32, 168192), sb("sg11", [128, 256], F32, 169216)]
    cbf = [sb("cbf0", [128, 256], BF16, 170240), sb("cbf1", [128, 256], BF16, 170752)]
    csq = [sb("csq0", [128, 256], BF16, 171264), sb("csq1", [128, 256], BF16, 171776)]
    c_t = sb("c_t", [128, 8, 256], F32, RF)
    y1 = sb("y1", [128, 8, 256], BF16, RF + 8192)
    ot = sb("ot", [128, 1024], F32, RF + 12288)
    sq1 = sb("sq1", [128, 1024], BF16, RH)
    h1b = sb("h1b", [128, 1024], BF16, RH + 2048)
    ct = sb("ct", [128, 256], F32, RH + 6144)
    ct2 = sb("ct2", [128, 256], F32, RH + 7168)

    CW, PADC, NLC = 272, 256, 10
    U_d = sb("U_d", [128, 4, 8, CW], BF16, RD)
    U_pd = sb("U_pd", [128, 4, 8, 768], BF16, RB)
    Xs_all = sb("Xs_all", [128, 8, 288], BF16, RA)
    BLx = [sb("BLx0", [128, 8, 128], BF16, RA + 4608), sb("BLx1", [128, 8, 128], BF16, RA + 6656),
           sb("BLx2", [128, 8, 128], BF16, RA + 66560), sb("BLx3", [128, 8, 128], BF16, RA + 28288)]
    mt4c = sb("mt4c", [128, NLC, 128], BF16, RA + 8704)
    t_a = sb("t_a", [128, 9, 16], F32, RA + 8704)
    t_b = sb("t_b", [128, 9, 16], F32, RA + 9280)
    MSL = []
    for i_ in range(3):
        o_ = RA + 9856 + i_ * 2176
        MSL.append(dict(Xa=sb("mXa%d" % i_, [128, 544], BF16, o_), Xb=sb("mXb%d" % i_, [128, 544], BF16, o_ + 1088)))
    MOPS = []
    for i_ in range(6):
        o_ = RA + 16384 + i_ * 4608
        MOPS.append(dict(SOPe=sb("mSOPe%d" % i_, [128, 8, 128], BF16, o_), MOPc=sb("mMOPc%d" % i_, [128, NLC, 128], BF16, o_ + 2048)))
    PSL = []
    POPS = []
    for i_ in range(4):
        o_ = RD + i_ * 8704
        POPS.append(dict(SOPe=sb("pSOPe%d" % i_, [128, 8, 128], BF16, o_), MOPc=sb("pMOPc%d" % i_, [128, NLC, 128], BF16, o_ + 2048)))
        PSL.append(dict(T0=sb("pT0%d" % i_, [128, 1024], BF16, o_ + 4608), Tb=sb("pTb%d" % i_, [128, 512], BF16, o_ + 6656),
                        Tc=sb("pTc%d" % i_, [128, 512], BF16, o_ + 7680)))
    for i_ in range(4):
        o_ = RA + 9856 + i_ * 4608
        POPS.append(dict(SOPe=sb("pSOPe%d" % (4 + i_), [128, 8, 128], BF16, o_), MOPc=sb("pMOPc%d" % (4 + i_), [128, NLC, 128], BF16, o_ + 2048)))
    COPs = sb("COPs", [128, 8, 9, 128], BF16, RA + 44032)
    B0pad = sb("B0pad", [128, 8, 128], BF16, RA + 62464)
    KOP = sb("KOP", [128, 8, 128], BF16, RA + 64512)
    bp1c = sb("bp1c", [128, 32, 16], F32, RH)
    bp2c = sb("bp2c", [128, 32, 16], F32, RH + 2048)
    cpc = sb("cpc", [128, 32, 16], F32, RH + 4096)
    cp2c = sb("cp2c", [128, 32, 16], F32, RH + 6144)
    Are = sb("Are", [128, 9, 32], F32, RH + 8192)
    Aim = sb("Aim", [128, 9, 32], F32, RH + 9344)
    sAre = sb("sAre", [128, 9, 32], F32, RH + 10496)
    nAim = sb("nAim", [128, 9, 32], F32, RH + 11648)
    kpre = sb("kpre", [128, 8, 32], F32, RH + 12800)
    nkpims = sb("nkpims", [128, 8, 32], F32, RH + 13824)
    k_im = sb("k_im", [128, 32], F32, RH + 14848)
    tmp8 = sb("tmp8", [128, 8, 32], F32, RH + 14976)
    psum = [st.enter_context(nc.psum_tensor("ps%d" % i, [128, 512], F32)) for i in range(8)]
    sems = [st.enter_context(nc.semaphore("s_" + e)) for e in Sched.ENG]
    dsems = [st.enter_context(nc.semaphore("dq%d" % i)) for i in range(Sched.NDMA)]
    S = Sched(sems, dsems)
    psrot = {"i": 0, "set": list(range(8))}
    del _MARKS[:]
    mark = lambda name: _MARKS.append((name, dict(S.count)))

    def PS():
        lst = psrot["set"]
        i = lst[psrot["i"] % len(lst)]
        psrot["i"] += 1
        return i

    rr = {"i": 0, "stg": 0, "rn": 0}

    def evac_eng():
        rr["i"] += 1
        return "act" if rr["i"] % 2 else "dve"

    def copy_op(eng, out, in_, reads, writes):
        if eng == "act":
            S.op("act", lambda e: e.activation(out=out, in_=in_, func=AF.Copy), reads, writes)
        else:
            S.op(eng, lambda e: e.tensor_copy(out=out, in_=in_), reads, writes)

    def tt(eng, out, in0, in1, op, reads, writes):
        S.op(eng, lambda e: e.tensor_tensor(out=out, in0=in0, in1=in1, op=op), reads, writes)

    def ts(eng, out, in0, s1, s2, op0, op1, reads, writes):
        if op1 is None:
            S.op(eng, lambda e: e.tensor_scalar(out=out, in0=in0, scalar1=s1, scalar2=None, op0=op0), reads, writes)
        else:
            S.op(eng, lambda e: e.tensor_scalar(out=out, in0=in0, scalar1=s1, scalar2=s2, op0=op0, op1=op1), reads, writes)

    def stt(eng, out, in0, sc, in1, op0, op1, reads, writes):
        S.op(eng, lambda e: e.scalar_tensor_tensor(out=out, in0=in0, scalar=sc, in1=in1, op0=op0, op1=op1), reads, writes)

    def act(out, in_, func, reads, writes, bias=None, scale=None, accum_out=None):
        kw = {}
        if bias is not None:
            kw["bias"] = bias
        if scale is not None:
            kw["scale"] = scale
        if accum_out is not None:
            kw["accum_out"] = accum_out
        S.op("act", lambda e: e.activation(out=out, in_=in_, func=func, **kw), reads, writes)

    def dma(out, in_, reads, writes, eng="sp"):
        return S.op(eng, lambda e: e.dma_start(out=out, in_=in_), reads, writes, dma=True)

    def load_w(dst, src2d, K, C, key, gcol=None, c_lo=0, k_lo=0, half=0):
        for k in range(k_lo, K):
            for c0 in range(0, C, 2048):
                cw = min(2048, C - c0)
                j = rr["stg"] % 2
                rr["stg"] += 1
                dma(stg[j][:, 0:cw], src2d[k * 128:(k + 1) * 128, c_lo + c0:c_lo + c0 + cw], [], ["stg%d" % j])
                hw = max(0, min(half - c0, cw))
                if hw > 0:
                    g1 = vecs[:, gcol + k:gcol + k + 1] if gcol is not None else 1.0
                    ts("dve", dst[:, k, c0:c0 + hw], stg[j][:, 0:hw], g1, 0.5, ALU.mult, ALU.mult, ["stg%d" % j, "vecs"], [key])
                if hw < cw:
                    o_, i_ = dst[:, k, c0 + hw:c0 + cw], stg[j][:, hw:cw]
                    if gcol is None:
                        copy_op(evac_eng(), o_, i_, ["stg%d" % j], [key])
                    elif evac_eng() == "act":
                        act(o_, i_, AF.Copy, ["stg%d" % j, "vecs"], [key], scale=vecs[:, gcol + k:gcol + k + 1])
                    else:
                        ts("dve", o_, i_, vecs[:, gcol + k:gcol + k + 1], None, ALU.mult, None, ["stg%d" % j, "vecs"], [key])

    def mm_group(bank, n, pairs, reads, col0=0):
        def fn(e):
            last = None
            for i, (l, r) in enumerate(pairs):
                last = e.matmul(psum[bank][:, col0:col0 + n], lhsT=l, rhs=r, start=(i == 0), stop=(i == len(pairs) - 1))
            return last
        S.op("pe", fn, reads, ["ps%d" % bank])

    def tap(name, ap, shape, dt, reads):
        if not DEBUG:
            return
        d = nc.dram_tensor("tap_" + name, shape, dt, kind="ExternalOutput").ap()
        taps[name] = dma(d, ap, reads, [])

    I32 = mybir.dt.int32
    MAGIC = 0x5f3759df

    def rsqrt_dve(y, x, t, yi, xi, keys, iters):
        ts("dve", yi, xi, 1, None, ALU.arith_shift_right, None, keys, keys)
        ts("dve", yi, yi, -1, MAGIC, ALU.mult, ALU.add, keys, keys)
        for _ in range(iters):
            tt("dve", t, y, y, ALU.mult, keys, keys)
            tt("dve", t, t, x, ALU.mult, keys, keys)
            ts("dve", t, t, -0.5, 1.5, ALU.mult, ALU.add, keys, keys)
            tt("dve", y, y, t, ALU.mult, keys, keys)

    ssi, rsi = ss.bitcast(I32), rs.bitcast(I32)

    def rstd_chain(src, srckeys, junk, junkkeys, iters=2, use_act=False):
        c = 4 + rr["rn"] % 4
        rr["rn"] += 1
        sk = "ss%d" % c
        xs, ys, tc = ss[:, c:c + 1], rs[:, c:c + 1], rt[:, c:c + 1]
        S.op("dve", lambda e: e.memset(ss[:, c:c + 1], 0.0), [], [sk])
        act(junk, src, AF.Square, srckeys, [sk] + junkkeys, accum_out=xs)
        ts("dve", xs, xs, 1.0 / 1024, 1e-6, ALU.mult, ALU.add, [sk], [sk])
        if use_act:
            act(ys, xs, AF.Sqrt, [sk], [sk])
            S.op("dve", lambda e: e.reciprocal(out=rs[:, c:c + 1], in_=rs[:, c:c + 1]), [sk], [sk])
            return ys, sk
        ts("dve", rsi[:, c:c + 1], ssi[:, c:c + 1], 1, None, ALU.arith_shift_right, None, [sk], [sk])
        ts("dve", rsi[:, c:c + 1], rsi[:, c:c + 1], -1, MAGIC, ALU.mult, ALU.add, [sk], [sk])
        for _ in range(iters):
            stt("dve", tc, ys, xs, ys, ALU.mult, ALU.mult, [sk], [sk])
            ts("dve", tc, tc, -0.5, 1.5, ALU.mult, ALU.add, [sk], [sk])
            tt("dve", ys, ys, tc, ALU.mult, [sk], [sk])
        return ys, sk

    def rmsnorm(src, srckeys, hout, hkey, use_act=False, scale_eng="act"):
        rcol, rk = rstd_chain(src, srckeys, hout, [hkey], use_act=use_act)
        if scale_eng == "act":
            act(hout, src, AF.Copy, srckeys + [rk], [hkey], scale=rcol)
        else:
            ts("dve", hout, src, rcol, None, ALU.mult, None, srckeys + [rk], [hkey])

    def make_rms_pipe(srcfn, scale_eng, junkbuf):
        st = {}

        def S0(i):
            pre = srcfn(i)[0]
            if pre is not None:
                pre()

        def S1(i):
            pre, src, srckeys, hout, hkey = srcfn(i)
            c = i % 4
            sk = "ss%d" % c
            S.op("dve", lambda e: e.memset(ss[:, c:c + 1], 0.0), [], [sk])
            act(junkbuf[:], src, AF.Square, srckeys, [sk, "junk"], accum_out=ss[:, c:c + 1])
            st[i] = (src, srckeys, hout, hkey, c, sk)

        def S2(i):
            src, srckeys, hout, hkey, c, sk = st[i]
            ts("dve", ss[:, c:c + 1], ss[:, c:c + 1], 1.0 / 1024, 1e-6, ALU.mult, ALU.add, [sk], [sk])
            act(rs[:, c:c + 1], ss[:, c:c + 1], AF.Sqrt, [sk], [sk])

        def S3(i):
            src, srckeys, hout, hkey, c, sk = st.pop(i)
            S.op("dve", lambda e: e.reciprocal(out=rs[:, c:c + 1], in_=rs[:, c:c + 1]), [sk], [sk])
            if scale_eng == "act":
                act(hout, src, AF.Copy, srckeys + [sk], [hkey], scale=rs[:, c:c + 1])
            else:
                ts("dve", hout, src, rs[:, c:c + 1], None, ALU.mult, None, srckeys + [sk], [hkey])

        def prologue(N):
            for i in range(-5, 0):
                step(i, N)

        def step(i, N):
            if 0 <= i + 5 < N:
                S0(i + 5)
            if 0 <= i + 4 < N:
                S1(i + 4)
            if 0 <= i + 3 < N:
                S2(i + 3)
            if 0 <= i + 2 < N:
                S3(i + 2)
        return prologue, step

    def final_norm(src, srckeys, gain_tile, gkey, sqbuf):
        rcol, rk = rstd_chain(src, srckeys, sqbuf[:], ["junk"], use_act=True)
        stt("dve", src, src, rcol, gain_tile[:], ALU.mult, ALU.mult, srckeys + [rk, gkey], srckeys)

    def transpose8(hsrc, hkey, hTdst, off, hTkey):
        for half in range(2):
            b = PS()
            def fn(e, half=half, b=b):
                last = None
                for k in range(4):
                    kk = half * 4 + k
                    last = e.matmul(psum[b][:, k * 128:(k + 1) * 128], lhsT=hsrc[:, kk * 128:(kk + 1) * 128], rhs=ident_bf[:], start=True, stop=True)
                return last
            S.op("pe", fn, [hkey, "ident_bf"], ["ps%d" % b])
            copy_op(evac_eng(), hTdst[:, half * 4:half * 4 + 4, off:off + 128],
                    psum[b][:, :].rearrange("p (a b) -> p a b", a=4), ["ps%d" % b], [hTkey])

    dma(ident_f[:], ident_d, [], ["ident_f"])
    dma(jswap_f[:], jswap_d, [], ["jswap_f"])
    dma(vecs[:], vecs_d, [], ["vecs"])
    copy_op("pool", ident_bf[:], ident_f[:], ["ident_f"], ["ident_bf"])
    S.op("pool", lambda e: e.memset(ones_bf[:], 1.0), [], ["ones_bf"])

    dgs = nc.dram_tensor("dgs", [8, 128, 31 * 128], BF16).ap()
    dstage = [sb("dstage0", [128, 31, 128], BF16, RD), sb("dstage1", [128, 31, 128], BF16, RD + 8192)]

    def diag_gen_fn():
        for m in range(8):
            dsb, dsk = dstage[m % 2], "dstage%d" % (m % 2)
            tt("dve", dsb[:, :, :], bass.AP(ident_f, 0, [[128, 128], [0, 31], [1, 128]]),
               bass.AP(vecs, V_CW + m * 31, [[NV, 128], [1, 31], [0, 128]]), ALU.mult, ["ident_f", "vecs"], [dsk])
            dma(dgs[m], dsb[:, :, :].rearrange("p a b -> p (a b)"), [dsk], ["dgs"])
            yield

    TB = token_blocks(512)
    P = ["prm"]
    sg1c = vecs[:, V_SG:V_SG + 1]
    def build_mop(g):
        idb = bass.AP(ident_f, 0, [[128, 128], [0, NLEV], [1, 128]])
        jsb = bass.AP(jswap_f, 0, [[128, 128], [0, NLEV], [1, 128]])
        v1b = bass.AP(Pre, g, [[13 * 32, 128], [32, NLEV], [0, 128]])
        v2b = bass.AP(v2, g, [[13 * 32, 128], [32, NLEV], [0, 128]])
        tt("dve", mt4[:, :, :], jsb, v2b, ALU.mult, ["jswap_f"] + P, ["mt4"])
        tt("dve", MOP[:, :, :], idb, v1b, ALU.mult, ["ident_f"] + P, ["MOP"])
        tt("dve", MOP[:, :, :], MOP[:, :, :], mt4[:, :, :], ALU.add, ["MOP", "mt4"], ["MOP"])

    def ssm_setup():
        dma(a_re[:], ssmA_d[:, 0, :], [], ["prm"])
        yield
        dma(a_im[:], ssmA_d[:, 1, :], [], ["prm"])
        yield
        dma(ldt[:], ssmA_d[:, 2, :], [], ["prm"])
        yield
        P = ["prm"]
        sg1c = vecs[:, V_SG:V_SG + 1]
        act(dt_t[:], ldt[:], AF.Exp, P, P)
        yield
        tt("dve", t0_[:], a_re[:], dt_t[:], ALU.mult, P, P)
        yield
        act(t0_[:], t0_[:], AF.Exp, P, P)
        yield
        tt("dve", t1_[:], a_im[:], dt_t[:], ALU.mult, P, P)
        yield
        def sin_of(dst, src, shift):
            ts("dve", dst, src, shift, 1.0 / TWO_PI, ALU.add, ALU.mult, P, P)
            copy_op("dve", qi[:], dst, P, P)
            copy_op("dve", qf[:], qi[:], P, P)
            ts("dve", dst, src, shift, None, ALU.add, None, P, P)
            stt("dve", dst, qf[:], -TWO_PI, dst, ALU.mult, ALU.add, P, P)
            ts("dve", qf[:], dst, math.pi, -TWO_PI, ALU.is_gt, ALU.mult, P, P)
            tt("dve", dst, dst, qf[:], ALU.add, P, P)
            ts("dve", qf[:], dst, -math.pi, TWO_PI, ALU.is_lt, ALU.mult, P, P)
            tt("dve", dst, dst, qf[:], ALU.add, P, P)
            act(dst, dst, AF.Sin, P, P)
        sin_of(t2_[:], t1_[:], 0.0)
        yield
        sin_of(t3_[:], t1_[:], 0.5 * math.pi)
        yield
        tt("dve", Pre[:, 0, :], t0_[:], t3_[:], ALU.mult, P, P)
        yield
        tt("dve", Pim[:, 0, :], t0_[:], t2_[:], ALU.mult, P, P)
        yield
        ts("dve", t0_[:], Pre[:, 0, :], -1.0, None, ALU.add, None, P, P)
        yield
        tt("dve", t1_[:], a_re[:], a_re[:], ALU.mult, P, P)
        yield
        tt("dve", t2_[:], a_im[:], a_im[:], ALU.mult, P, P)
        yield
        tt("dve", t1_[:], t1_[:], t2_[:], ALU.add, P, P)
        yield
        S.op("dve", lambda e: e.reciprocal(out=t1_[:], in_=t1_[:]), P, P)
        yield
        tt("dve", t2_[:], t0_[:], a_re[:], ALU.mult, P, P)
        yield
        tt("dve", t3_[:], Pim[:, 0, :], a_im[:], ALU.mult, P, P)
        yield
        tt("dve", t2_[:], t2_[:], t3_[:], ALU.add, P, P)
        yield
        tt("dve", k_re[:], t2_[:], t1_[:], ALU.mult, P, P)
        yield
        tt("dve", t2_[:], Pim[:, 0, :], a_re[:], ALU.mult, P, P)
        yield
        tt("dve", t3_[:], t0_[:], a_im[:], ALU.mult, P, P)
        yield
        tt("dve", t2_[:], t2_[:], t3_[:], ALU.subtract, P, P)
        yield
        tt("dve", t2_[:], t2_[:], t1_[:], ALU.mult, P, P)
        yield
        copy_op("dve", k_im[:], t2_[:], P, P)
        yield
        ts("dve", nkims[:], t2_[:], sg1c, -1.0, ALU.mult, ALU.mult, P + ["vecs"], P)
        yield
        for k in range(12):
            tt("dve", t0_[:], Pre[:, k, :], Pre[:, k, :], ALU.mult, P, P)
            yield
            tt("dve", t1_[:], Pim[:, k, :], Pim[:, k, :], ALU.mult, P, P)
            yield
            tt("dve", Pre[:, k + 1, :], t0_[:], t1_[:], ALU.subtract, P, P)
            yield
            tt("dve", t0_[:], Pre[:, k, :], Pim[:, k, :], ALU.mult, P, P)
            yield
            ts("dve", Pim[:, k + 1, :], t0_[:], 2.0, None, ALU.mult, None, P, P)
            yield
        ts("dve", v2[:, :, :], Pim[:, :, :], sg1c, None, ALU.mult, None, P + ["vecs"], P)
        yield
        ts("dve", c1ims[:], v2[:, 11, :], -1.0, None, ALU.mult, None, P, P)
        yield
        ts("dve", c2ims[:], v2[:, 12, :], -1.0, None, ALU.mult, None, P, P)
        yield
        S.op("dve", lambda e: e.memset(Are[:, 0, :], 1.0), [], P)
        yield
        S.op("dve", lambda e: e.memset(Aim[:, 0, :], 0.0), [], P)
        yield
        copy_op("dve", Are[:, 1, :], Pre[:, 0, :], P, P)
        yield
        copy_op("dve", Aim[:, 1, :], Pim[:, 0, :], P, P)
        yield
        for e_ in range(1, 8):
            tt("dve", t0_[:], Are[:, e_, :], Pre[:, 0, :], ALU.mult, P, P)
            yield
            tt("dve", t1_[:], Aim[:, e_, :], Pim[:, 0, :], ALU.mult, P, P)
            yield
            tt("dve", Are[:, e_ + 1, :], t0_[:], t1_[:], ALU.subtract, P, P)
            yield
            tt("dve", t0_[:], Are[:, e_, :], Pim[:, 0, :], ALU.mult, P, P)
            yield
            tt("dve", t1_[:], Aim[:, e_, :], Pre[:, 0, :], ALU.mult, P, P)
            yield
            tt("dve", Aim[:, e_ + 1, :], t0_[:], t1_[:], ALU.add, P, P)
            yield
        kre_b = bass.AP(k_re, 0, [[32, 128], [0, 8], [1, 32]])
        kim_b = bass.AP(k_im, 0, [[32, 128], [0, 8], [1, 32]])
        tt("dve", kpre[:, :, :], Are[:, 0:8, :], kre_b, ALU.mult, P, P)
        yield
        tt("dve", tmp8[:, :, :], Aim[:, 0:8, :], kim_b, ALU.mult, P, P)
        yield
        tt("dve", kpre[:, :, :], kpre[:, :, :], tmp8[:, :, :], ALU.subtract, P, P)
        yield
        tt("dve", nkpims[:, :, :], Are[:, 0:8, :], kim_b, ALU.mult, P, P)
        yield
        tt("dve", tmp8[:, :, :], Aim[:, 0:8, :], kre_b, ALU.mult, P, P)
        yield
        tt("dve", nkpims[:, :, :], nkpims[:, :, :], tmp8[:, :, :], ALU.add, P, P)
        yield
        ts("dve", nkpims[:, :, :], nkpims[:, :, :], sg1c, -1.0, ALU.mult, ALU.mult, P + ["vecs"], P)
        yield
        ts("dve", sAre[:, :, :], Are[:, :, :], sg1c, None, ALU.mult, None, P + ["vecs"], P)
        yield
        ts("dve", nAim[:, :, :], Aim[:, :, :], -1.0, None, ALU.mult, None, P, P)
        yield
        dma(bp1c[:], bp1c_d, [], P)
        yield
        dma(bp2c[:], bp2c_d, [], P)
        yield
        dma(cpc[:], cpc_d, [], P)
        yield
        dma(cp2c[:], cp2c_d, [], P)
        yield
        tt("dve", Eb[:, :], ident_f[:, 0:64], ident_f[:, 64:128], ALU.add, ["ident_f"], ["Eb"])
        yield
        for (lo_, hi_, s0, s1_) in ((0, 64, Pre, v2), (64, 128, v2, Pre)):
            copy_op("dve", Vh[lo_:hi_, :, :, 0], s0[lo_:hi_, 3:13, :].rearrange("p k g -> p g k"), P, ["Vh"])
            yield
            copy_op("dve", Vh[lo_:hi_, :, :, 1], s1_[lo_:hi_, 3:13, :].rearrange("p k g -> p g k"), P, ["Vh"])
            yield

    def build_dve(g, main, SL, sl, kp, bsl):
        gj = g % 8
        c0 = gj * 16
        BL, blk_ = BLx[bsl], "BLx%d" % bsl
        o = BL[:, :, c0:c0 + 16]
        tt("dve", t_a[:, 0:8, :], bass.AP(bp1c, g * 16, [[512, 128], [0, 8], [1, 16]]), bass.AP(kpre, g, [[256, 128], [32, 8], [0, 16]]), ALU.mult, P, ["t_a"])
        tt("dve", t_b[:, 0:8, :], bass.AP(bp2c, g * 16, [[512, 128], [0, 8], [1, 16]]), bass.AP(nkpims, g, [[256, 128], [32, 8], [0, 16]]), ALU.mult, P, ["t_b"])
        tt("dve", o, t_a[:, 0:8, :], t_b[:, 0:8, :], ALU.add, ["t_a", "t_b"], [blk_])
        if main:
            copy_op("dve", B0pad[:, gj, c0:c0 + 16], BL[:, 0, c0:c0 + 16], [blk_], ["B0pad"])
            tt("dve", t_a[:, :, :], bass.AP(cpc, g * 16, [[512, 128], [0, 9], [1, 16]]), bass.AP(sAre, g, [[288, 128], [32, 9], [0, 16]]), ALU.mult, P, ["t_a"])
            tt("dve", t_b[:, :, :], bass.AP(cp2c, g * 16, [[512, 128], [0, 9], [1, 16]]), bass.AP(nAim, g, [[288, 128], [32, 9], [0, 16]]), ALU.mult, P, ["t_b"])
            tt("dve", COPs[:, gj, :, c0:c0 + 16], t_a[:, :, :], t_b[:, :, :], ALU.add, ["t_a", "t_b"], ["COPs"])
        mk = "%sMOPc%d" % (kp, sl)
        tt("dve", SL["MOPc"][:, :, :].rearrange("p k (h m) -> p k h m", h=2),
           bass.AP(Eb, 0, [[64, 128], [0, NLC], [0, 2], [1, 64]]),
           bass.AP(Vh, g * NLC * 2, [[32 * NLC * 2, 128], [2, NLC], [1, 2], [0, 64]]), ALU.mult, ["Eb", "Vh"], [mk])

    def build_pe(g, SL, sl, kp, bsl):
        c0 = (g % 8) * 16
        BL, blk_ = BLx[bsl], "BLx%d" % bsl
        sk_ = "%sSOPe%d" % (kp, sl)
        for half in range(2):
            b = PS()
            def fn(e, half=half, b=b):
                last = None
                for k in range(4):
                    last = e.matmul(psum[b][:, k * 128:(k + 1) * 128], lhsT=BL[:, half * 4 + k, :], rhs=ident_bf[:], start=True, stop=True)
                return last
            S.op("pe", fn, [blk_, "ident_bf"], ["ps%d" % b])
            copy_op("act", SL["SOPe"][:, half * 4:half * 4 + 4, :], psum[b][:, :].rearrange("p (a b) -> p a b", a=4), ["ps%d" % b], [sk_])
        S.op("dve", lambda e: e.memset(BL[:, :, c0:c0 + 16], 0.0), [], [blk_])

    def p_build_dve(gs, ob):
        for sl, g in enumerate(gs):
            build_dve(g, False, POPS[ob + sl], ob + sl, "p", sl)

    def p_build_pe(gs, ob):
        for sl, g in enumerate(gs):
            build_pe(g, POPS[ob + sl], ob + sl, "p", sl)

    def p_compute(gs, ob):
        for sl, g in enumerate(gs):
            blk, SL, OP = g // 8, PSL[sl], POPS[ob + sl]
            for cb in (0, 384):
                b = PS()
                mm_group(b, 384, [(OP["SOPe"][:, 7 - i, :], U_pd[:, blk, i, cb:cb + 384]) for i in range(8)], ["pSOPe%d" % (ob + sl), "U_pd"])
                copy_op("dve", SL["T0"][:, 256 + cb:256 + cb + 384], psum[b][:, 0:384], ["ps%d" % b], ["T0_%d" % sl])

    def p_levels(gs, ob):
        N, pst = 1024, 1024
        for k in range(10):
            Nh = N // 2
            for sl, g in enumerate(gs):
                SL, OP = PSL[sl], POPS[ob + sl]
                src, skey = (SL["T0"], "T0_%d" % sl) if k == 0 else ((SL["Tb"], "Tb_%d" % sl) if k % 2 == 1 else (SL["Tc"], "Tc_%d" % sl))
                dst, dkey = (SL["Tb"], "Tb_%d" % sl) if k % 2 == 0 else (SL["Tc"], "Tc_%d" % sl)
                b = PS()
                odd = bass.AP(src, 1, [[pst, 128], [2, Nh]])
                even = bass.AP(src, 0, [[pst, 128], [2, Nh]])
                mm_group(b, Nh, [(ident_bf[:], odd), (OP["MOPc"][:, k, :], even)], ["ident_bf", "pMOPc%d" % (ob + sl), skey])
                if k < 9:
                    copy_op("act", dst[:, 0:Nh], psum[b][:, 0:Nh], ["ps%d" % b], [dkey])
                else:
                    copy_op("act", Xin[:, g:g + 1], psum[b][:, 0:1], ["ps%d" % b], ["Xin"])
            N, pst = Nh, 512

    def m_build_dve(gs, ob):
        for sl, g in enumerate(gs):
            build_dve(g, True, MOPS[ob + sl], ob + sl, "m", sl)

    def m_build_pe(gs, ob):
        for sl, g in enumerate(gs):
            build_pe(g, MOPS[ob + sl], ob + sl, "m", sl)

    def m_compute(gs, ob):
        for sl, g in enumerate(gs):
            blk, SL, OP = g // 8, MSL[sl], MOPS[ob + sl]
            b = PS()
            mm_group(b, CW, [(OP["SOPe"][:, 7 - i, :], U_d[:, blk, i, 0:CW]) for i in range(8)], ["mSOPe%d" % (ob + sl), "U"])
            copy_op("dve", SL["Xa"][:, PADC + 1:PADC + 1 + CW], psum[b][:, 0:CW], ["ps%d" % b], ["Xa%d" % sl])
            copy_op("act", SL["Xa"][:, PADC:PADC + 1], Xin[:, g:g + 1], ["Xin"], ["Xa%d" % sl])

    def m_levels(gs, ob):
        for k in range(9):
            sh = 1 << k
            for sl, g in enumerate(gs):
                SL, OP, gj = MSL[sl], MOPS[ob + sl], g % 8
                src, sk = (SL["Xa"], "Xa%d" % sl) if k % 2 == 0 else (SL["Xb"], "Xb%d" % sl)
                b = PS()
                mm_group(b, CW + 1, [(ident_bf[:], src[:, PADC:PADC + CW + 1]), (OP["MOPc"][:, k, :], src[:, PADC - sh:PADC - sh + CW + 1])],
                         ["ident_bf", "mMOPc%d" % (ob + sl), sk])
                if k < 8:
                    dst, dk = (SL["Xb"], "Xb%d" % sl) if k % 2 == 0 else (SL["Xa"], "Xa%d" % sl)
                    copy_op("act", dst[:, PADC:PADC + CW + 1], psum[b][:, 0:CW + 1], ["ps%d" % b], [dk])
                else:
                    copy_op("act", Xs_all[:, gj, 0:CW + 1], psum[b][:, 0:CW + 1], ["ps%d" % b], ["Xs"])

    w_in0s = sb("w_in0s", [128, 8, 512], BF16, RA + 41984)
    hT_p = [sb("hT_p0", [128, 8, 512], BF16, RA + 50176), sb("hT_p1", [128, 8, 512], BF16, RA + 58368)]
    xt_p = [sb("xtp%d" % i_, [128, 1024], F32, 118784 + 4096 * i_) for i_ in range(3)]
    h_p = [sb("h_p%d" % i_, [128, 1024], BF16, 131072 + 2048 * i_) for i_ in range(3)]
    load_w(w_in0s, w_in0, 8, 512, "w_in0s", gcol=V_G0, c_lo=512)
    setup_gen = ssm_setup()
    mark('pre_setup')
    for i_ in range(4):
        S.op("dve", lambda e, i_=i_: e.memset(BLx[i_][:, :, :], 0.0), [], ["BLx%d" % i_])
    ptiles = [(sp, t) for sp in range(3) for t in range(16)]
    diag_gen = diag_gen_fn()

    xt_p4 = xt_p + [stg[1][:, 0:1024]]
    xk_p4 = ["xtp0", "xtp1", "xtp2", "stg1"]

    def p_src(i):
        sp, t = ptiles[i]
        xb, xk = xt_p4[i % 4], xk_p4[i % 4]
        return (lambda: dma(xb[:], xpre[sp, t * 128:(t + 1) * 128, :], [], [xk])), xb[:], [xk], h_p[i % 3][:], "h_p%d" % (i % 3)

    junk_p = sb("junk_p", [128, 1024], BF16, 137216)
    p_pro, p_step = make_rms_pipe(p_src, "dve", junk_p)

    def pB(i):
        bi = i // 4
        transpose8(h_p[i % 3], "h_p%d" % (i % 3), hT_p[bi % 2], (i % 4) * 128, "hTp%d_%d" % (bi % 2, i % 4))

    def pC(bi):
        sp, t0 = bi // 4, (bi % 4) * 512
        hTb = hT_p[bi % 2]
        for m in range(4):
            b = PS()
            mm_group(b, 512, [(w_in0s[:, k, m * 128:(m + 1) * 128], hTb[:, k, 0:512]) for k in range(8)],
                     ["w_in0s"] + ["hTp%d_%d" % (bi % 2, q) for q in range(4)])
            cb = sp * 256 + t0 // 8
            copy_op(evac_eng(), U_pd[:, m, :, cb:cb + 64], psum[b][:, 0:512].rearrange("p (c i) -> p i c", i=8), ["ps%d" % b], ["U_pd"])

    p_pro(48)
    for i in range(48):
        pB(i)
        p_step(i, 48)
        if i % 4 == 3:
            pC(i // 4)
        for _ in range(8):
            next(setup_gen, None)
        if i % 4 == 1:
            next(diag_gen, None)
    for _ in setup_gen:
        pass
    for _ in diag_gen:
        pass
    S.barrier()
    for i_ in range(4):
        S.op("dve", lambda e, i_=i_: e.memset(PSL[i_]["T0"][:, 0:256], 0.0), [], ["T0_%d" % i_])
    mark('pre_tiles')
    pbat = [[g0, g0 + 1, g0 + 2, g0 + 3] for g0 in range(0, 32, 4)]
    p_build_dve(pbat[0], 0)
    p_build_pe(pbat[0], 0)
    for k_, gs_ in enumerate(pbat):
        ob_ = (k_ % 2) * 4
        p_compute(gs_, ob_)
        if k_ + 1 < len(pbat):
            p_build_dve(pbat[k_ + 1], ((k_ + 1) % 2) * 4)
        p_levels(gs_, ob_)
        if k_ + 1 < len(pbat):
            p_build_pe(pbat[k_ + 1], ((k_ + 1) % 2) * 4)
    tap("Xin", Xin[:], [128, 32], F32, ["Xin"])
    mark('pre_ssm')
    S.barrier()
    load_w(w_in0_t, w_in0, 8, 2048, "w_in0", gcol=V_G0)
    for g in range(4):
        j = rr["stg"] % 2
        rr["stg"] += 1
        dma(stg[j][:, 0:128], pool_w[g], [], ["stg%d" % j])
        copy_op(evac_eng(), pool_w_t[:, g, :], stg[j][:, 0:128], ["stg%d" % j], ["pool_w"])
    S.op("pool", lambda e: e.memset(upool[:, :, 0:16], 0.0), [], ["upool"])
    xt3 = [xt[0], xt[1], sb("xt2", [128, 1024], F32, RE + 1024)]
    ht3 = [h_t, h_t2, sb("h_t3", [128, 1024], BF16, RE + 5120)]

    pl2 = [pl, sb("pl_b", [128, 512], BF16, RE + 14464)]

    def pool_chain(t0, n, g):
        if True:
            plb, plk = pl2[g % 2], "pl%d" % (g % 2)
            w = 2 << g
            u = lambda lo, hi, g=g: upool[:, g, t0 + lo:t0 + hi]
            N = 16 + n
            if g == 0:
                tt("dve", wb[:, 16:N], u(16, N), u(15, N - 1), ALU.add, ["upool"], ["wb"])
            elif g == 1:
                tt("dve", wa[:, 14:N], u(14, N), u(13, N - 1), ALU.add, ["upool"], ["wa"])
                tt("dve", wb[:, 16:N], wa[:, 16:N], wa[:, 14:N - 2], ALU.add, ["wa"], ["wb"])
            elif g == 2:
                tt("dve", wa[:, 10:N], u(10, N), u(9, N - 1), ALU.add, ["upool"], ["wa"])
                tt("dve", wb[:, 12:N], wa[:, 12:N], wa[:, 10:N - 2], ALU.add, ["wa"], ["wb"])
                tt("dve", wa[:, 16:N], wb[:, 16:N], wb[:, 12:N - 4], ALU.add, ["wb"], ["wa"])
                copy_op("dve", wb[:, 16:N], wa[:, 16:N], ["wa"], ["wb"])
            else:
                tt("dve", wa[:, 2:N], u(2, N), u(1, N - 1), ALU.add, ["upool"], ["wa"])
                tt("dve", wb[:, 4:N], wa[:, 4:N], wa[:, 2:N - 2], ALU.add, ["wa"], ["wb"])
                tt("dve", wa[:, 8:N], wb[:, 8:N], wb[:, 4:N - 4], ALU.add, ["wb"], ["wa"])
                tt("dve", wb[:, 16:N], wa[:, 16:N], wa[:, 8:N - 8], ALU.add, ["wa"], ["wb"])
            stt("dve", plb[:, 0:n], wb[:, 16:N], 1.0 / w, u(16, N), ALU.mult, ALU.subtract, ["wb", "upool"], [plk])
            if t0 == 0:
                tt("dve", ptmp[:, :], wb[:, 16 + 128:16 + 144], vecs[:, V_INV + g * 16:V_INV + g * 16 + 16], ALU.mult, ["wb", "vecs"], ["ptmp"])
                tt("dve", plb[:, 128:144], ptmp[:, :], u(16 + 128, 16 + 144), ALU.subtract, ["ptmp", "upool", plk], [plk])

    def pool_mix(t0, n, g):
        plb, plk = pl2[g % 2], "pl%d" % (g % 2)
        b = PS()
        mm_group(b, n, [(pool_w_t[:, g, :], plb[:, 0:n])], ["pool_w", plk])
        stt("dve", yg[:, g, t0:t0 + n], psum[b][:, 0:n], vecs[:, V_PSC + g:V_PSC + g + 1], s_t[:, g, t0:t0 + n],
            ALU.mult, ALU.mult, ["ps%d" % b, "vecs", "s"], ["yg"])

    def pool_sched(blk, m):
        t0, n = blk
        if m == 4:
            pool_chain(t0, n, 0)
            pool_chain(t0, n, 1)
        elif m == 8:
            pool_mix(t0, n, 0)
            pool_mix(t0, n, 1)
            pool_chain(t0, n, 2)
            pool_chain(t0, n, 3)
        elif m == 12:
            pool_mix(t0, n, 2)
            pool_mix(t0, n, 3)

    xt4b = xt3 + [stg[1][:, 0:1024]]
    xk4b = ["xt0", "xt1", "xt2", "stg1"]

    def m_src(i):
        xb, xk = xt4b[i % 4], xk4b[i % 4]
        return (lambda: dma(xb[:], xin[i * 128:(i + 1) * 128, :], [], [xk])), xb[:], [xk], ht3[i % 3][:], "h%d" % (i % 3)

    junk_m = sb("junk_m", [128, 1024], BF16, RE + 7168)
    m_pro, m_step = make_rms_pipe(m_src, "dve", junk_m)

    def mB(i):
        transpose8(ht3[i % 3], "h%d" % (i % 3), hT, (i % 4) * 128, "hT_%d" % (i % 4))

    m_pro(NT)
    prev_blk = None
    for (t0, n) in token_blocks(512):
        for ti in range(n // 128):
            tile_i = t0 // 128 + ti
            mB(tile_i)
            m_step(tile_i, NT)
        for m in range(16):
            if prev_blk is not None:
                pool_sched(prev_blk, m)
            b = PS()
            mm_group(b, n, [(w_in0_t[:, k, m * 128:(m + 1) * 128], hT[:, k, 0:n]) for k in range(8)], ["w_in0"] + ["hT_%d" % q for q in range(n // 128)])
            pk = "ps%d" % b
            if m < 4:
                copy_op("act", upool[:, m, 16 + t0:16 + t0 + n], psum[b][:, 0:n], [pk], ["upool"])
            elif m < 8:
                copy_op("dve", U_d[:, m - 4, :, t0 // 8:(t0 + n) // 8], psum[b][:, 0:n].rearrange("p (c i) -> p i c", i=8), [pk], ["U"])
            else:
                act(s_t[:, m - 8, t0:t0 + n], psum[b][:, 0:n], AF.Silu, [pk], ["s"])
        prev_blk = (t0, n)
    for m_ in (4, 8, 12):
        pool_sched(prev_blk, m_)
    tap("ygp", yg[:, 0, :], [128, T], BF16, ["yg"])
    tap("s", s_t[:, 0, :], [128, T], BF16, ["s"])
    S.barrier()

    for i_ in range(3):
        S.op("dve", lambda e, i_=i_: e.memset(BLx[i_][:, :, :], 0.0), [], ["BLx%d" % i_])
    S.op("dve", lambda e: e.memset(B0pad[:, :, :], 0.0), [], ["B0pad"])
    S.op("dve", lambda e: e.memset(COPs[:, :, :, :], 0.0), [], ["COPs"])
    for i_ in range(3):
        S.op("dve", lambda e, i_=i_: e.memset(MSL[i_]["Xa"][:, 0:PADC], 0.0), [], ["Xa%d" % i_])
        S.op("dve", lambda e, i_=i_: e.memset(MSL[i_]["Xb"][:, 0:PADC], 0.0), [], ["Xb%d" % i_])
    for blk in range(4):
        mbat = [[blk * 8 + 0, blk * 8 + 1, blk * 8 + 2], [blk * 8 + 3, blk * 8 + 4, blk * 8 + 5], [blk * 8 + 6, blk * 8 + 7]]
        m_build_dve(mbat[0], 0)
        m_build_pe(mbat[0], 0)
        for k_, gs_ in enumerate(mbat):
            ob_ = (k_ % 2) * 3
            m_compute(gs_, ob_)
            if k_ + 1 < 3:
                m_build_dve(mbat[k_ + 1], ((k_ + 1) % 2) * 3)
            m_levels(gs_, ob_)
            if k_ + 1 < 3:
                m_build_pe(mbat[k_ + 1], ((k_ + 1) % 2) * 3)
        for half in range(2):
            b = PS()
            def fnk(e, half=half, b=b):
                last = None
                for tq in range(4):
                    for gj in range(8):
                        last = e.matmul(psum[b][:, tq * 128:(tq + 1) * 128], lhsT=B0pad[:, gj, :], rhs=COPs[:, gj, half * 4 + tq, :],
                                        start=(gj == 0), stop=(gj == 7))
                return last
            S.op("pe", fnk, ["B0pad", "COPs"], ["ps%d" % b])
            copy_op(evac_eng(), KOP[:, half * 4:half * 4 + 4, :], psum[b][:, :].rearrange("p (a b) -> p a b", a=4), ["ps%d" % b], ["KOP"])
        ysv = ysm[:, blk, :].rearrange("p (c i) -> p i c", i=8)
        for j in range(8):
            b = PS()
            pairs = [(KOP[:, j - i, :], U_d[:, blk, i, 0:CW]) for i in range(j + 1)]
            pairs += [(COPs[:, gj, j + 1, :], Xs_all[:, gj, 0:CW]) for gj in range(8)]
            mm_group(b, CW, pairs, ["KOP", "U", "COPs", "Xs"])
            stt("dve", ysv[:, j, :], U_d[:, blk, j, 0:CW], vecs[:, V_SSD + blk:V_SSD + blk + 1], psum[b][:, 0:CW],
                ALU.mult, ALU.add, ["U", "vecs", "ps%d" % b], ["ysm"])
    tap("ysm", ysm[:, 0, :], [128, T], BF16, ["ysm"])
    S.barrier()

    load_w(w_glu_t, w_glu, 4, 1024, "w_glu", half=512)
    load_w(w_out0_t, w_out0, 8, 1024, "w_out0")
    for (t0, n) in TB:
        for m in range(4):
            bv, bg = PS(), PS()
            mm_group(bv, n, [(w_glu_t[:, k, m * 128:(m + 1) * 128], ysm[:, k, t0:t0 + n]) for k in range(4)], ["w_glu", "ysm"])
            mm_group(bg, n, [(w_glu_t[:, k, 512 + m * 128:512 + (m + 1) * 128], ysm[:, k, t0:t0 + n]) for k in range(4)], ["w_glu", "ysm"])
            sg = sgt[m % 2]
            sgk = "sgt%d" % (m % 2)
            act(sg[:, 0:n], psum[bg][:, 0:n], AF.Tanh, ["ps%d" % bg], [sgk], scale=0.5)
            stt("dve", sg[:, 0:n], sg[:, 0:n], 1.0, psum[bv][:, 0:n], ALU.add, ALU.mult, ["ps%d" % bv, sgk], [sgk])
            tt("dve", yg[:, 4 + m, t0:t0 + n], sg[:, 0:n], s_t[:, 4 + m, t0:t0 + n], ALU.mult, [sgk, "s"], ["yg"])
    S.barrier()

    load_w(w_in1_t, w_in1, 5, 3072, "w_in1", gcol=V_G1, half=1024)
    for ti in range(NT):
        xb = xt4[ti % 2]
        xk = "xt4%d" % (ti % 2)
        dma(xb[:], xin[ti * 128:(ti + 1) * 128, :], [], [xk])
        for half in range(2):
            b = PS()
            mm_group(b, 512, [(yg[:, k, ti * 128:(ti + 1) * 128], w_out0_t[:, k, half * 512:(half + 1) * 512]) for k in range(8)], ["yg", "w_out0"])
            tt("dve", x1[:, ti, half * 512:(half + 1) * 512], psum[b][:, :], xb[:, half * 512:(half + 1) * 512], ALU.add,
               ["ps%d" % b, xk], ["x1"])
    tap("x1", x1[:, 1, :], [128, 1024], F32, ["x1"])
    S.barrier()

    dma(gainf[:], gains[2], [], ["gainf"])
    rr["stg"] = 0
    stg1 = [sb("stgL0", [128, 2048], F32, RF), sb("stgL1", [128, 2048], F32, RF + 8192)]
    load_w(w_in1_t, w_in1, 8, 3072, "w_in1", gcol=V_G1, k_lo=5, half=1024)
    load_w(w_out1_t, w_out1, 8, 1024, "w_out1")
    S.barrier()
    S.op("pool", lambda e: e.memset(g_t[:, :, 0:30], 0.0), [], ["g"])
    h13 = [h1, h1b, sb("h1c", [128, 1024], BF16, RH + 4096)]

    def l_src(i):
        return None, x1[:, i, :], ["x1"], h13[i % 3][:], "h1_%d" % (i % 3)

    l_pro, l_step = make_rms_pipe(l_src, "act", sq1)

    def lB(i):
        transpose8(h13[i % 3], "h1_%d" % (i % 3), hT1, (i % 2) * 128, "hT1_%d" % (i % 2))

    LB = token_blocks(256)
    ct2s = [ct2, sb("ct2b", [128, 256], F32, 172288)]

    def hkeys(n):
        return ["hT1_%d" % q for q in range(n // 128)]

    def prep(bi):
        t0, n = LB[bi]
        for ti in range(n // 128):
            tile_i = t0 // 128 + ti
            lB(tile_i)
            l_step(tile_i, NT)

    def in_vg(bi, m):
        t0, n = LB[bi]
        bv, bg = PS(), PS()
        mm_group(bv, n, [(w_in1_t[:, k, m * 128:(m + 1) * 128], hT1[:, k, 0:n]) for k in range(8)], ["w_in1"] + hkeys(n))
        mm_group(bg, n, [(w_in1_t[:, k, 1024 + m * 128:1024 + (m + 1) * 128], hT1[:, k, 0:n]) for k in range(8)], ["w_in1"] + hkeys(n))
        sg = sg1[m % 2]
        sgk = "sg1%d" % (m % 2)
        act(sg[:, 0:n], psum[bg][:, 0:n], AF.Tanh, ["ps%d" % bg], [sgk], scale=0.5)
        stt("dve", g_t[:, m, 30:30 + n], sg[:, 0:n], 1.0, psum[bv][:, 0:n], ALU.add, ALU.mult, ["ps%d" % bv, sgk], ["g"])

    def in_z(bi, m):
        t0, n = LB[bi]
        b = PS()
        mm_group(b, n, [(w_in1_t[:, k, 2048 + m * 128:2048 + (m + 1) * 128], hT1[:, k, 0:n]) for k in range(8)], ["w_in1"] + hkeys(n))
        act(s1[:, m, 0:n], psum[b][:, 0:n], AF.Silu, ["ps%d" % b], ["s1_%d" % m])

    pend_stores = []

    def flush_stores():
        for f_ in pend_stores:
            f_()
        del pend_stores[:]

    def dg_load(m):
        dma(diag2[m % 2][:, :, :].rearrange("p a b -> p (a b)"), dgs[m][:, TD * 128:31 * 128], ["dgs"], ["diag%d" % (m % 2)])

    ot2 = [ot, sb("otb", [128, 1024], F32, 155904)]

    def conv(bi):
        t0, n = LB[bi]
        bs_, bq_ = PS(), PS()
        pend = None
        for m in range(8):
            dg, dgk = diag2[m % 2], "diag%d" % (m % 2)
            b = PS()
            while b in (bs_, bq_):
                b = PS()
            mm_group(b, n, [(dg[:, k - TD, :], g_t[:, m, k:k + n]) for k in range(TD, 31)], [dgk, "g"])
            if m + 2 < 8:
                dg_load(m + 2)
            ca, cak = cacc[m % 2], "cacc%d" % (m % 2)
            ts("dve", ca[:, 0:n], g_t[:, m, 0:n], vecs[:, V_CW + m * 31:V_CW + m * 31 + 1], None, ALU.mult, None, ["g", "vecs"], [cak])
            for k in range(1, TD):
                stt("dve", ca[:, 0:n], g_t[:, m, k:k + n], vecs[:, V_CW + m * 31 + k:V_CW + m * 31 + k + 1], ca[:, 0:n], ALU.mult, ALU.add,
                    ["g", "vecs", cak], [cak])
            cbc = vecs[:, V_CB + m:V_CB + m + 1]
            stt("dve", c_t[:, m, 0:n], psum[b][:, 0:n], cbc, ca[:, 0:n], ALU.add, ALU.add, ["ps%d" % b, "vecs", cak], ["c%d" % m])
            cb_, cq_ = cbf[m % 2], csq[m % 2]
            cbk, cqk = "cbf%d" % (m % 2), "csq%d" % (m % 2)
            copy_op("act", cb_[:, 0:n], c_t[:, m, 0:n], ["c%d" % m], [cbk])
            act(cq_[:, 0:n], c_t[:, m, 0:n], AF.Square, ["c%d" % m], [cqk])
            def stats(m=m, cb_=cb_, cq_=cq_, cbk=cbk, cqk=cqk):
                def fn1(e):
                    return e.matmul(psum[bs_][:, 0:n], lhsT=ones_bf[:], rhs=cb_[:, 0:n], start=(m == 0), stop=(m == 7))
                S.op("pe", fn1, ["ones_bf", cbk], ["ps%d" % bs_])
                def fn2(e):
                    return e.matmul(psum[bq_][:, 0:n], lhsT=ones_bf[:], rhs=cq_[:, 0:n], start=(m == 0), stop=(m == 7))
                S.op("pe", fn2, ["ones_bf", cqk], ["ps%d" % bq_])
            if pend is not None:
                pend()
            pend = stats
        pend()
        copy_op("dve", g_t[:, :, 0:30], g_t[:, :, n:n + 30], ["g"], ["g"])
        flush_stores()
        return bs_, bq_

    def ln_head(bi, bs_, bq_):
        t0, n = LB[bi]
        ts("dve", mean[:, 0:n], psum[bs_][:, 0:n], 1.0 / 1024, None, ALU.mult, None, ["ps%d" % bs_], ["mean"])
        tt("dve", ct[:, 0:n], mean[:, 0:n], mean[:, 0:n], ALU.mult, ["mean"], ["ct"])
        stt("dve", rstd[:, 0:n], psum[bq_][:, 0:n], 1.0 / 1024, ct[:, 0:n], ALU.mult, ALU.subtract, ["ps%d" % bq_, "ct"], ["rstd"])
        ts("dve", rstd[:, 0:n], rstd[:, 0:n], 1e-5, None, ALU.add, None, ["rstd"], ["rstd"])
        act(rstd[:, 0:n], rstd[:, 0:n], AF.Sqrt, ["rstd"], ["rstd"])
        S.op("dve", lambda e, n=n: e.reciprocal(out=rstd[:, 0:n], in_=rstd[:, 0:n]), ["rstd"], ["rstd"])

    def norm_m(bi, m):
        t0, n = LB[bi]
        c2, c2k = ct2s[m % 2], "ct2_%d" % (m % 2)
        tt("dve", ct[:, 0:n], c_t[:, m, 0:n], mean[:, 0:n], ALU.subtract, ["c%d" % m, "mean"], ["ct"])
        tt("dve", c2[:, 0:n], ct[:, 0:n], rstd[:, 0:n], ALU.mult, ["ct", "rstd"], [c2k])
        act(c2[:, 0:n], c2[:, 0:n], AF.Silu, [c2k, "vecs"], [c2k], bias=vecs[:, V_LB + m:V_LB + m + 1], scale=vecs[:, V_LG + m:V_LG + m + 1])
        tt("dve", y1[:, m, 0:n], c2[:, 0:n], s1[:, m, 0:n], ALU.mult, [c2k, "s1_%d" % m], ["y1"])

    def outp(bi):
        t0, n = LB[bi]
        for ti in range(n // 128):
            tile_i = t0 // 128 + ti
            otb, otk = ot2[tile_i % 2], "ot%d" % (tile_i % 2)
            for half in range(2):
                b = PS()
                mm_group(b, 512, [(y1[:, k, ti * 128:(ti + 1) * 128], w_out1_t[:, k, half * 512:(half + 1) * 512]) for k in range(8)], ["y1", "w_out1"])
                tt("dve", otb[:, half * 512:(half + 1) * 512], psum[b][:, :], x1[:, tile_i, half * 512:(half + 1) * 512], ALU.add,
                   ["ps%d" % b, "x1"], [otk])
            if tile_i >= 1:
                final_norm(otb[:], [otk], gainf, "gainf", sq1)
                pend_stores.append(lambda tile_i=tile_i, otb=otb, otk=otk: dma(out_d[(tile_i - 1) * 128:tile_i * 128, :], otb[:], [otk], ["out"]))

    l_pro(NT)
    prep(0)
    for m in range(8):
        in_vg(0, m)
        in_z(0, m)
    dg_load(0)
    dg_load(1)
    for bi in range(len(LB)):
        bs_, bq_ = conv(bi)
        nxt = bi + 1 < len(LB)
        if nxt:
            prep(bi + 1)
        ln_head(bi, bs_, bq_)
        for m in range(8):
            if nxt:
                in_vg(bi + 1, m)
            if m < 4:
                norm_m(bi, 2 * m)
                norm_m(bi, 2 * m + 1)
            if nxt:
                in_z(bi + 1, m)
            if m == 4:
                if nxt:
                    dg_load(0)
                    dg_load(1)
                outp(bi)
    flush_stores()
    S.barrier()

    with nc.Block() as block:
        @block.sync
        def _(e):
            S.replay("sp", e)

        @block.scalar
        def _(e):
            S.replay("act", e)

        @block.vector
        def _(e):
            S.replay("dve", e)

        @block.gpsimd
        def _(e):
            S.replay("pool", e)

        @block.tensor
        def _(e):
            S.replay("pe", e)
    st.close()
    return nc, taps


def _prep_inputs(inp):
    f = lambda a: np.ascontiguousarray(np.asarray(a, dtype=np.float32))
    x = f(inp["x"])
    gains = np.stack([np.broadcast_to(f(inp["even_norm"])[0], (128, 1024)),
                      np.broadcast_to(f(inp["odd_norm"])[0], (128, 1024)),
                      np.broadcast_to(f(inp["final_norm"]), (128, 1024))]).copy()
    a_re, a_im, ldt = f(inp["ssm_a_re"])[0], f(inp["ssm_a_im"])[0], f(inp["ssm_log_dt"])[0]
    ssmA = np.zeros((128, 3, 32), np.float32)
    ssmA[:, 0, :] = np.concatenate([a_re.T, a_re.T], 0)
    ssmA[:, 1, :] = np.concatenate([a_im.T, a_im.T], 0)
    ssmA[:, 2, :] = ldt[None, :]
    b_re, b_im = f(inp["ssm_b_re"])[0], f(inp["ssm_b_im"])[0]
    c_re, c_im = f(inp["ssm_c_re"])[0], f(inp["ssm_c_im"])[0]
    bp1 = np.zeros((128, 32, 128), np.float32)
    bp2 = np.zeros((128, 32, 128), np.float32)
    cp = np.zeros((128, 32, 128), np.float32)
    for g in range(32):
        c0 = (g % 8) * 16
        bp1[0:64, g, c0:c0 + 16] = b_re[g]
        bp1[64:128, g, c0:c0 + 16] = b_im[g]
        bp2[0:64, g, c0:c0 + 16] = b_im[g]
        bp2[64:128, g, c0:c0 + 16] = b_re[g]
        cp[0:64, g, c0:c0 + 16] = c_re[g].T
        cp[64:128, g, c0:c0 + 16] = c_im[g].T
    bp1c = np.zeros((128, 32, 16), np.float32)
    bp2c = np.zeros((128, 32, 16), np.float32)
    cpc = np.zeros((128, 32, 16), np.float32)
    cp2c = np.zeros((128, 32, 16), np.float32)
    for g in range(32):
        bp1c[0:64, g], bp1c[64:128, g] = b_re[g], b_im[g]
        bp2c[0:64, g], bp2c[64:128, g] = b_im[g], b_re[g]
        cpc[0:64, g], cpc[64:128, g] = c_re[g].T, c_im[g].T
        cp2c[0:64, g], cp2c[64:128, g] = c_im[g].T, c_re[g].T
    ident = np.eye(128, dtype=np.float32)
    jswap = np.zeros((128, 128), np.float32)
    for k in range(128):
        jswap[k, (k + 64) % 128] = 1.0
    vbase = np.zeros((128, NV), np.float32)
    vbase[:, V_PSC:V_PSC + 4] = f(inp["pool_scale"])[0].reshape(4, 128).T
    vbase[:, V_SSD:V_SSD + 4] = f(inp["ssm_d"])[0].reshape(4, 128).T
    vbase[:, V_CB:V_CB + 8] = f(inp["conv_b"])[0].reshape(8, 128).T
    vbase[:, V_LG:V_LG + 8] = f(inp["conv_ln_g"])[0].reshape(8, 128).T
    vbase[:, V_LB:V_LB + 8] = f(inp["conv_ln_b"])[0].reshape(8, 128).T
    vbase[:, V_CW:V_CW + 248] = f(inp["conv_w"])[0].reshape(31, 8, 128).transpose(2, 1, 0).reshape(128, 248)
    vbase[:, V_G0:V_G0 + 8] = f(inp["even_norm"])[0].reshape(8, 128).T
    vbase[:, V_G1:V_G1 + 8] = f(inp["odd_norm"])[0].reshape(8, 128).T
    vbase[0:64, V_SG] = 1.0
    vbase[64:128, V_SG] = -1.0
    common = {
        "w_in0": f(inp["even_w_in"])[0], "w_glu": f(inp["ssm_w_glu"])[0], "w_out0": f(inp["even_w_out"])[0],
        "pool_w": f(inp["pool_w"])[0], "w_in1": f(inp["odd_w_in"])[0], "w_out1": f(inp["odd_w_out"])[0],
        "gains": gains, "ssmA": ssmA, "bp1": bp1, "bp2": bp2, "cp": cp, "bp1c": bp1c, "bp2c": bp2c, "cpc": cpc, "cp2c": cp2c, "ident": ident, "jswap": jswap,
    }
    maps = []
    for c in range(8):
        b, q = c // 4, c % 4
        t0 = q * 2048
        xin = np.zeros((T, 1024), np.float32)
        if q == 0:
            xin[128:] = x[b, 0:2048]
        else:
            xin[:] = x[b, t0 - 128:t0 + 2048]
        v = vbase.copy()
        for g in range(4):
            w = 2 << g
            for i in range(16):
                v[:, V_INV + g * 16 + i] = (1.0 / min(i + 1, w)) if q == 0 else (1.0 / w)
        for qq in range(q):
            v[:, V_SEL + (b * 4 + qq) * 3 + (q - 1 - qq)] = 1.0
        m = dict(common)
        xpre = np.zeros((3, 2048, 1024), np.float32)
        for sp in range(3):
            lo = t0 - 128 - (3 - sp) * 2048
            hi = lo + 2048
            if hi > 0:
                l2 = max(lo, 0)
                xpre[sp, l2 - lo:] = x[b, l2:hi]
        m["xpre"] = xpre
        m["xin"] = xin
        m["vecs"] = v
        maps.append(m)
    return maps


_NC_CACHE = {}


def kernel(**inputs):
    maps = _prep_inputs(inputs)
    if "nc" not in _NC_CACHE:
        _NC_CACHE["nc"] = build_nc()
    nc, taps = _NC_CACHE["nc"]
    res = run_bass_kernel_spmd(nc, maps, core_ids=list(range(8)))
    out = np.zeros((2, 8192, 1024), np.float32)
    for c in range(8):
        b, q = c // 4, c % 4
        out[b, q * 2048:(q + 1) * 2048] = res.results[c]["out"]
    if DEBUG:
        kernel.last = res
    return out
```
